# Optimizing a Trainium2 kernel written in Bass

```python
import jax, jax.numpy as jnp
from jax import lax
import numpy as np

D_MODEL = 1024
BATCH = 8
SEQ = 2048
DEPTH = 1
DEC_BATCH = 128
DEC_SEQ = 4
PAST_LEN = 16384
PAGE_SIZE = 128

HEAD_DIM = 64
N_HEADS = D_MODEL // HEAD_DIM
D_RWKV = N_HEADS * HEAD_DIM
D_CONV = D_MODEL
CONV_W = 3
LORA_W = 64
LORA_A = 64
RMS_EPS = 1e-6
GN_EPS = 64e-5
CONV_SIZES = [D_CONV, D_CONV, D_CONV, D_CONV]
RWKV_SIZES = [D_RWKV, D_RWKV, D_RWKV, LORA_W, LORA_A, D_RWKV]
N_CONV_COLS = sum(CONV_SIZES)
N_RWKV_COLS = sum(RWKV_SIZES)
N_IN = N_CONV_COLS + N_RWKV_COLS + 2 * D_MODEL

kernel_name = "hybrid_shortconv_rwkv7_gated_merge_step"


def _split(p, sizes):
    idx = np.cumsum(sizes)[:-1].tolist()
    return jnp.split(p, idx, axis=-1)


def _rmsnorm(x, g):
    xf = x.astype(jnp.float32)
    y = xf * lax.rsqrt(jnp.mean(xf * xf, axis=-1, keepdims=True) + RMS_EPS)
    return (y * g.astype(jnp.float32)).astype(x.dtype)


def _rwkv7_scan(S0, r, k, v, decay, kk, a):
    def step(S, inp):
        r_t, k_t, v_t, d_t, kk_t, a_t = inp
        sa = jnp.einsum('bhvk,bhk->bhv', S, -kk_t)
        S = (S * d_t[:, :, None, :] + sa[..., None] * (kk_t * a_t)[:, :, None, :]
             + v_t[..., None] * k_t[:, :, None, :])
        y = jnp.einsum('bhvk,bhk->bhv', S, r_t)
        return S, y
    xs = tuple(jnp.moveaxis(t.astype(jnp.float32), 1, 0) for t in (r, k, v, decay, kk, a))
    S_fin, ys = lax.scan(step, S0.astype(jnp.float32), xs)
    return S_fin, jnp.moveaxis(ys, 0, 1)


def _layer(x, conv_buf, h_last, S0, norm_g, w_in, conv_w, mu_shift, w0, w_up, a0, a_up,
           k_k, k_a, r_k, ln_w, ln_b, w_out_c, w_out_r, w_o):
    b, s, _ = x.shape
    h = _rmsnorm(x, norm_g)
    p = jnp.einsum('bsd,dn->bsn', h, w_in)
    p_conv, p_rwkv, p_gate = _split(p, [N_CONV_COLS, N_RWKV_COLS, 2 * D_MODEL])

    xin, bg, cg, zc = _split(p_conv, CONV_SIZES)
    u = cg * xin
    u_ext = jnp.concatenate([conv_buf.astype(u.dtype), u], axis=1)
    conv = sum(conv_w[j] * u_ext[:, j:j + s] for j in range(CONV_W))
    y_c = bg * conv * jax.nn.silu(zc)
    new_conv = u_ext[:, -(CONV_W - 1):]

    w_r = w_in[:, N_CONV_COLS:N_CONV_COLS + N_RWKV_COLS]
    p_last = jnp.einsum('bd,dn->bn', h_last.astype(h.dtype), w_r)
    prev = jnp.concatenate([p_last[:, None], p_rwkv[:, :-1]], axis=1)
    xm = p_rwkv + (prev - p_rwkv) * mu_shift
    r, k, v, wd, ad, zr = _split(xm, RWKV_SIZES)
    w_logit = w0 + jnp.einsum('bsr,rc->bsc', jnp.tanh(wd), w_up)
    w_log = (-jax.nn.softplus(-w_logit.astype(jnp.float32)) - 0.5)
    decay = jnp.exp(-jnp.exp(w_log))
    a = jax.nn.sigmoid((a0 + jnp.einsum('bsr,rc->bsc', ad, a_up)).astype(jnp.float32))
    hs = (b, s, N_HEADS, HEAD_DIM)
    r_h = r.reshape(hs).astype(jnp.float32)
    v_h = v.reshape(hs).astype(jnp.float32)
    k_f = k.astype(jnp.float32)
    kk = (k_f * k_k).reshape(hs)
    kk = kk / jnp.maximum(jnp.sqrt(jnp.sum(kk * kk, axis=-1, keepdims=True)), 1e-12)
    k_h = (k_f * (1.0 + (a - 1.0) * k_a)).reshape(hs)
    a_h = a.reshape(hs)
    S_fin, ys = _rwkv7_scan(S0, r_h, k_h, v_h, decay.reshape(hs), kk, a_h)
    mu = jnp.mean(ys, axis=-1, keepdims=True)
    var = jnp.mean(jnp.square(ys - mu), axis=-1, keepdims=True)
    yn = ((ys - mu) * lax.rsqrt(var + GN_EPS)).reshape(b, s, D_RWKV) * ln_w + ln_b
    bonus = (jnp.sum(r_h * k_h * r_k, axis=-1, keepdims=True) * v_h).reshape(b, s, D_RWKV)
    y_r = ((yn + bonus) * jax.nn.silu(zr.astype(jnp.float32))).astype(x.dtype)

    gc, gr = _split(p_gate, [D_MODEL, D_MODEL])
    pc = jnp.einsum('bsc,cd->bsd', y_c, w_out_c)
    pr = jnp.einsum('bsc,cd->bsd', y_r, w_out_r)
    m = jax.nn.sigmoid(gc) * pc + jax.nn.sigmoid(gr) * pr
    out = jnp.einsum('bsd,de->bse', m, w_o)
    return x + out, new_conv, h[:, -1], S_fin.astype(S0.dtype)


def setup_inputs(seed: int = 0) -> dict:
    key = jax.random.key(seed)
    ks = jax.random.split(key, 24)
    f = jnp.float32
    n = lambda i, shape, sc: jax.random.normal(ks[i], shape, f) * sc
    L = DEPTH
    return {
        "x_prompt": n(0, (BATCH, SEQ, D_MODEL), 1.0),
        "x_sample": n(1, (DEC_BATCH, DEC_SEQ, D_MODEL), 1.0),
        "state_conv": n(2, (L, DEC_BATCH, CONV_W - 1, D_CONV), 0.5),
        "state_shift": n(3, (L, DEC_BATCH, D_MODEL), 1.0),
        "state_rwkv": n(4, (L, DEC_BATCH, N_HEADS, HEAD_DIM, HEAD_DIM), 0.3),
        "norm_g": 1.0 + n(5, (L, D_MODEL), 0.05),
        "w_in": n(6, (L, D_MODEL, N_IN), D_MODEL ** -0.5),
        "conv_w": n(7, (L, CONV_W, D_CONV), CONV_W ** -0.5),
        "mu_shift": jax.random.uniform(ks[8], (L, N_RWKV_COLS), f),
        "w0": jax.random.uniform(ks[9], (L, D_RWKV), f, -4.0, 1.0),
        "w_up": n(10, (L, LORA_W, D_RWKV), 0.5 * LORA_W ** -0.5),
        "a0": n(11, (L, D_RWKV), 0.5),
        "a_up": n(12, (L, LORA_A, D_RWKV), 0.5 * LORA_A ** -0.5),
        "k_k": 0.85 + n(13, (L, D_RWKV), 0.05),
        "k_a": 1.0 + n(14, (L, D_RWKV), 0.05),
        "r_k": n(15, (L, N_HEADS, HEAD_DIM), 0.1),
        "ln_w": 1.0 + n(16, (L, D_RWKV), 0.05),
        "ln_b": n(17, (L, D_RWKV), 0.02),
        "w_out_c": n(18, (L, D_CONV, D_MODEL), D_CONV ** -0.5),
        "w_out_r": n(19, (L, D_RWKV, D_MODEL), D_RWKV ** -0.5),
        "w_o": n(20, (L, D_MODEL, D_MODEL), D_MODEL ** -0.5),
        "final_g": 1.0 + n(21, (D_MODEL,), 0.05),
    }


def reference(x_prompt, x_sample, state_conv, state_shift, state_rwkv, norm_g, w_in, conv_w,
              mu_shift, w0, w_up, a0, a_up, k_k, k_a, r_k, ln_w, ln_b, w_out_c, w_out_r, w_o,
              final_g):
    bp = x_prompt.shape[0]
    xp, xs = x_prompt, x_sample
    cp, sp, rp, cs, ss, rs = [], [], [], [], [], []
    for l in range(DEPTH):
        wl = (norm_g[l], w_in[l], conv_w[l], mu_shift[l], w0[l], w_up[l], a0[l], a_up[l],
              k_k[l], k_a[l], r_k[l], ln_w[l], ln_b[l], w_out_c[l], w_out_r[l], w_o[l])
        conv0 = jnp.zeros((bp, CONV_W - 1, D_CONV), x_prompt.dtype)
        shift0 = jnp.zeros((bp, D_MODEL), x_prompt.dtype)
        S00 = jnp.zeros((bp, N_HEADS, HEAD_DIM, HEAD_DIM), state_rwkv.dtype)
        xp, c1, s1, r1 = _layer(xp, conv0, shift0, S00, *wl)
        xs, c2, s2, r2 = _layer(xs, state_conv[l], state_shift[l], state_rwkv[l], *wl)
        cp.append(c1); sp.append(s1); rp.append(r1)
        cs.append(c2); ss.append(s2); rs.append(r2)
    y_prompt = _rmsnorm(xp, final_g)
    y_sample = _rmsnorm(xs, final_g)
    return (y_prompt, y_sample, jnp.stack(cp), jnp.stack(sp), jnp.stack(rp),
            jnp.stack(cs), jnp.stack(ss), jnp.stack(rs))
```

```python
import numpy as np
from contextlib import ExitStack
import concourse.bass as bass
import concourse.mybir as mybir
from concourse.bass_utils import run_bass_kernel_spmd

F32 = mybir.dt.float32
BF16 = mybir.dt.bfloat16
AF = mybir.ActivationFunctionType
ALU = mybir.AluOpType

ENGS = ("tensor", "vector", "scalar", "gpsimd", "sync")
SEM_EPOCH = 30000


class _Op:
    __slots__ = ("eng", "fn", "reads", "writes", "dma", "deps", "pos", "marked", "ev", "dkey", "waits", "barriered")

    def __init__(self, eng, fn, reads, writes, dma, dkey):
        self.eng, self.fn, self.reads, self.writes, self.dma, self.dkey = eng, fn, reads, writes, dma, dkey
        self.deps = []
        self.marked = False
        self.ev = None
        self.waits = []
        self.barriered = False


class _DB:
    def __init__(self, tiles, prog, attr="par"):
        self.tiles, self.prog, self.attr = tiles, prog, attr

    def __getitem__(self, idx):
        return self.tiles[getattr(self.prog, self.attr)][idx]


class Prog:
    def __init__(self, nc):
        self.nc = nc
        self.ops = []
        self.last_w = {}
        self.readers = {}
        self.barrier_deps = {e: [] for e in ENGS}
        self.per_eng = {e: [] for e in ENGS}
        self.dma_ops = []
        self.par = 0
        self.dbkeys = set()
        self.cpar = 0
        self.cdbkeys = set()

    def op(self, eng, fn, reads=(), writes=(), dma=False, dkey=None):
        km = lambda k: ("%s@%d" % (k, self.par)) if k in self.dbkeys else (("%s#%d" % (k, self.cpar)) if k in self.cdbkeys else k)
        o = _Op(eng, fn, tuple(km(k) for k in reads), tuple(km(k) for k in writes), dma, dkey)
        deps = set()
        for k in o.reads:
            w = self.last_w.get(k)
            if w is not None:
                deps.add(w)
        for k in o.writes:
            w = self.last_w.get(k)
            if w is not None:
                deps.add(w)
            for r in self.readers.get(k, ()):
                deps.add(r)
        for d in self.barrier_deps[eng]:
            deps.add(d)
        self.barrier_deps[eng] = []
        o.deps = list(deps)
        for k in o.writes:
            self.last_w[k] = o
            self.readers[k] = []
        for k in o.reads:
            if k not in o.writes:
                self.readers.setdefault(k, []).append(o)
        o.pos = len(self.per_eng[eng])
        self.per_eng[eng].append(o)
        self.ops.append(o)
        if dma:
            self.dma_ops.append(o)
        return o

    def barrier(self):
        lasts = []
        for e in ENGS:
            for o in reversed(self.per_eng[e]):
                if not o.dma and o.fn is not None:
                    lasts.append(o)
                    break
        lasts += [o for o in self.dma_ops if not o.barriered]
        for o in self.dma_ops:
            o.barriered = True
        for e in ENGS:
            self.barrier_deps[e] = list(lasts)
        self.last_w = {}
        self.readers = {}

    def finish(self, final_eng="sync"):
        o = self.op(final_eng, None)
        o.deps = list(self.dma_ops)

    def emit(self):
        nc = self.nc
        waited_pos = {e: {x: -1 for x in ENGS} for e in ENGS}
        dma_cnt = {}
        waited_dma = {e: {} for e in ENGS}
        for o in self.ops:
            if o.dma:
                dma_cnt[o.dkey] = dma_cnt.get(o.dkey, 0) + 16
                o.ev = ("d_%s" % o.dkey, dma_cnt[o.dkey])
                o.marked = True
        for o in self.ops:
            e = o.eng
            need = []
            for d in o.deps:
                if d.dma:
                    nm, val = d.ev
                    if waited_dma[e].get(nm, 0) < val:
                        waited_dma[e][nm] = val
                        need.append(d)
                else:
                    if d.fn is None:
                        continue
                    if d.eng == e and e == "tensor":
                        continue
                    if waited_pos[e][d.eng] < d.pos:
                        waited_pos[e][d.eng] = d.pos
                        need.append(d)
            o.waits = need
            for d in need:
                d.marked = True
        cnt = {e: 0 for e in ENGS}
        for o in self.ops:
            if o.dma or not o.marked:
                continue
            cnt[o.eng] += 1
            ep, tk = divmod(cnt[o.eng] - 1, SEM_EPOCH)
            o.ev = ("s_%s_%d" % (o.eng, ep), tk + 1)
        sems = {}
        with ExitStack() as st:
            for o in self.ops:
                if o.marked and o.ev[0] not in sems:
                    sems[o.ev[0]] = st.enter_context(nc.semaphore(o.ev[0]))
            with nc.Block() as block:
                def run(engname):
                    def body(eng):
                        for o in self.per_eng[engname]:
                            best = {}
                            for d in o.waits:
                                nm, val = d.ev
                                if best.get(nm, 0) < val:
                                    best[nm] = val
                            for nm, val in best.items():
                                eng.wait_ge(sems[nm], val)
                            if o.fn is None:
                                continue
                            ins = o.fn(eng)
                            if o.marked:
                                ins.then_inc(sems[o.ev[0]], 16 if o.dma else 1)
                    return body
                block.tensor(run("tensor"))
                block.vector(run("vector"))
                block.scalar(run("scalar"))
                block.gpsimd(run("gpsimd"))
                block.sync(run("sync"))
        return len(sems)


D = 1024
NIN = 10368
SEQ = 2048
NSB = 16
C = 64
G = 128
SC0 = 1 + SEQ
HCOLS = SC0 + 80
NTOK = SEQ + 64
LD = 0.6065306597126334
RMS_EPS = 1e-6
GN_EPS = 64e-5
CB_XIN, CB_B, CB_C, CB_ZC = 0, 1024, 2048, 3072
CB_R = 4096
CB_GC, CB_GR = 8320, 9344

IN_SPECS = [
    ("xp", [SEQ, D]), ("xs", [64, D]), ("sconv", [32, D]), ("sshift", [16, D]),
    ("srwkv", [NSB, 16, 64, 64]), ("norm_g", [D]), ("w_in", [D, NIN]), ("conv_w", [3, D]),
    ("mu", [4224]), ("w0", [D]), ("w_up", [64, D]), ("a0", [D]), ("a_up", [64, D]),
    ("k_k", [D]), ("k_a", [D]), ("r_k", [D]), ("ln_w", [D]), ("ln_b", [D]),
    ("w_out_c", [D, D]), ("w_out_r", [D, D]), ("w_o", [D, D]), ("final_g", [D]),
]
OUT_SPECS = [
    ("yp", [SEQ, D]), ("ys", [64, D]), ("ncp", [2, D]), ("nsp", [1, D]), ("nrp", [16, 64, 64]),
    ("ncs", [32, D]), ("nss", [16, D]), ("nrs", [NSB, 16, 64, 64]),
]


def build_program(stage=99, dbg=False):
    nc = bass.Bass("TRN2", target_bir_lowering=False)
    I = {n: nc.dram_tensor(n, s, F32, kind="ExternalInput").ap() for n, s in IN_SPECS}
    O = {n: nc.dram_tensor(n, s, F32, kind="ExternalOutput").ap() for n, s in OUT_SPECS}
    DBG = {}
    P = Prog(nc)

    def op(eng, fn, r=(), w=()):
        return P.op(eng, fn, reads=r, writes=w)

    def dma(out, in_, r=(), w=(), dkey=None, eng="sync"):
        return P.op("sync", lambda e: e.dma_start(out=out, in_=in_), reads=r, writes=w, dma=True, dkey=dkey)

    def mm(out, lhsT, rhs, start, stop, r, w):
        return P.op("tensor", lambda e: e.matmul(out, lhsT=lhsT, rhs=rhs, start=start, stop=stop), reads=r, writes=w)

    def tr(out, in_, ident, r, w):
        return P.op("tensor", lambda e: e.transpose(out, in_, ident), reads=r, writes=w)

    def act(out, in_, func, r, w, bias=None, scale=None, accum=None):
        kw = {}
        if bias is not None:
            kw["bias"] = bias
        if scale is not None:
            kw["scale"] = scale
        if accum is not None:
            kw["accum_out"] = accum
        return P.op("scalar", lambda e: e.activation(out=out, in_=in_, func=func, **kw), reads=r, writes=w)

    def tt(eng, out, in0, in1, o, r, w):
        return P.op(eng, lambda e: e.tensor_tensor(out=out, in0=in0, in1=in1, op=o), reads=r, writes=w)

    def ts(eng, out, in0, s1, s2, o0, o1, r, w):
        return P.op(eng, lambda e: e.tensor_scalar(out=out, in0=in0, scalar1=s1, scalar2=s2, op0=o0, op1=o1), reads=r, writes=w)

    def ts1(eng, out, in_, sc, o, r, w):
        return P.op(eng, lambda e: e.tensor_single_scalar(out=out, in_=in_, scalar=sc, op=o), reads=r, writes=w)

    def stt(eng, out, in0, sc, in1, o0, o1, r, w):
        return P.op(eng, lambda e: e.scalar_tensor_tensor(out=out, in0=in0, scalar=sc, in1=in1, op0=o0, op1=o1), reads=r, writes=w)

    def cp(eng, out, in_, r, w):
        if eng == "scalar":
            return P.op(eng, lambda e: e.copy(out=out, in_=in_), reads=r, writes=w)
        return P.op(eng, lambda e: e.tensor_copy(out=out, in_=in_), reads=r, writes=w)

    def mset(eng, ap, val, w):
        return P.op(eng, lambda e: e.memset(ap, val), writes=w)

    with ExitStack() as top:
        def sb(name, shape, dt=F32, st=top):
            return st.enter_context(nc.sbuf_tensor(name, shape, dt))

        psb = [top.enter_context(nc.psum_tensor("psb%d" % i, [128, 512], F32)) for i in range(8)]
        pstate = {"i": 0}

        def psum():
            i = pstate["i"]
            pstate["i"] = (i + 1) % 8
            return psb[i], "ps%d" % i

        ident = sb("ident", [128, 128])
        bones = sb("bones", [128, 128])
        hT = sb("hT", [128, 8, HCOLS], BF16)
        prm = sb("prm", [128, 128])
        ucb = sb("ucb", [128, 8, 32])

        mset("gpsimd", ident[:], 1.0, ["ident"])
        op("gpsimd", lambda e: e.affine_select(out=ident[:], in_=ident[:], pattern=[[-1, 128]], compare_op=ALU.is_equal,
                                               fill=0.0, base=0, channel_multiplier=1), ["ident"], ["ident"])
        mset("gpsimd", bones[:], 0.0, ["bones"])
        mset("gpsimd", bones[0:64, 0:64], 1.0, ["bones"])
        mset("gpsimd", bones[64:128, 64:128], 1.0, ["bones"])
        mset("gpsimd", hT[:, :, 0:1], 0.0, ["hT0"])

        PR_MU, PR_W0, PR_A0, PR_KK, PR_KA, PR_RK, PR_LNW, PR_LNB, PR_CW = 0, 33, 41, 49, 57, 65, 73, 81, 89
        with ExitStack() as s0:
            prow = sb("prow", [128, 128], st=s0)
            mset("gpsimd", prow[:], 0.0, ["prow"])
            dma(prow[0:33, :], I["mu"].rearrange("(r c) -> r c", c=128), r=["prow"], w=["prow"], dkey="prow")
            for nm, base in (("w0", PR_W0), ("a0", PR_A0), ("k_k", PR_KK), ("k_a", PR_KA), ("r_k", PR_RK),
                             ("ln_w", PR_LNW), ("ln_b", PR_LNB)):
                dma(prow[base:base + 8, :], I[nm].rearrange("(r c) -> r c", c=128), r=["prow"], w=["prow"], dkey="prow")
            dma(prow[PR_CW:PR_CW + 24, :], I["conv_w"].rearrange("j (r c) -> (j r) c", c=128), r=["prow"], w=["prow"], dkey="prow")
            pb, pk = psum()
            tr(pb[:, 0:128], prow[:], ident[:], ["prow", "ident"], [pk])
            cp("vector", prm[:], pb[:, 0:128], [pk], ["prm"])
            P.barrier()

        sAr = ExitStack()
        arena = sAr.enter_context(nc.sbuf_tensor("arena", [128, 13056], F32))
        wr = arena[:, :].bitcast(BF16)[:, 0:17408].rearrange("p (d n) -> p d n", n=2176)
        wst = [arena[:, 8704 + i * 2176:8704 + (i + 1) * 2176] for i in range(2)]

        def pass_nbl(hp):
            return [(CB_R + (4 * hp + b) * 128, 4 * hp + b) for b in range(4)] \
                + [(CB_R + 1024 + (4 * hp + b) * 128, 8 + 4 * hp + b) for b in range(4)] \
                + [(CB_R + 2048 + (4 * hp + b) * 128, 16 + 4 * hp + b) for b in range(4)] \
                + [(CB_R + 3072, 24)] \
                + [(CB_R + 3200 + (4 * hp + b) * 128, 25 + 4 * hp + b) for b in range(4)]

        def load_wr_dma(hp, db):
            nbl = pass_nbl(hp)
            s = db % 2
            kw = "wst%d" % s
            rows = slice(db * 128, (db + 1) * 128)
            for (lo, n, col) in ((0, 512, nbl[0][0]), (512, 512, nbl[4][0]), (1024, 512, nbl[8][0]),
                                 (1536, 128, nbl[12][0]), (1664, 512, nbl[13][0])):
                dma(wst[s][:, lo:lo + n], I["w_in"][rows, col:col + n], w=[kw], dkey=kw)

        def load_wr_cast(hp, db):
            s = db % 2
            kw = "wst%d" % s
            cp("scalar", wr[:, db, 0:1088], wst[s][:, 0:1088], [kw], ["wr"])
            cp("vector", wr[:, db, 1088:2176], wst[s][:, 1088:2176], [kw], ["wr"])

        def load_wr(hp, db):
            load_wr_dma(hp, db)
            load_wr_cast(hp, db)

        with ExitStack() as s1:
            gbc = sb("gbc", [128, D], st=s1)
            dma(gbc[:], I["norm_g"].partition_broadcast(128), w=["gbc"], dkey="gbc")
            xt = [sb("xt%d" % i, [128, D], st=s1) for i in range(4)]
            ht = [sb("ht%d" % i, [128, D], st=s1) for i in range(2)]
            junk = sb("junk", [128, D], st=s1)
            ssum = [sb("ssum%d" % i, [128, 1], st=s1) for i in range(2)]
            rstd = [sb("rstd%d" % i, [128, 1], st=s1) for i in range(2)]
            def stage_a(i):
                if i % 2 == 0 and i < 16:
                    load_wr_dma(0, i // 2)
                if i % 2 == 0 and 2 <= i:
                    load_wr_cast(0, i // 2 - 1)
                s = i % 2
                sx = i % 4
                kx, kh, ks, kr = "xt%d" % sx, "ht%d" % s, "ss%d" % s, "rs%d" % s
                if i < 16:
                    dma(xt[sx][:], I["xp"][i * 128:(i + 1) * 128, :], w=[kx], dkey=kx)
                else:
                    mset("gpsimd", xt[sx][:], 0.0, [kx])
                    dma(xt[sx][0:64, :], I["xs"], r=[kx], w=[kx], dkey=kx)
                    dma(xt[sx][64:80, :], I["sshift"], r=[kx], w=[kx], dkey=kx)
                    dma(xt[sx][80:112, :], I["sconv"], r=[kx], w=[kx], dkey=kx)
                mset("gpsimd", ssum[s][:], 0.0, [ks])
                act(junk[:], xt[sx][:], AF.Square, [kx, ks], [ks], accum=ssum[s][:])
                ts("vector", rstd[s][:], ssum[s][:], 1.0 / D, RMS_EPS, ALU.mult, ALU.add, [ks], [kr])
                act(rstd[s][:], rstd[s][:], AF.Sqrt, [kr], [kr])
                op("vector", lambda e, s=s: e.reciprocal(out=rstd[s][:], in_=rstd[s][:]), [kr], [kr])
                if i < 16:
                    stt("vector", ht[s][:], xt[sx][:], rstd[s][:, 0:1], gbc[:], ALU.mult, ALU.mult, [kx, kr, "gbc"], [kh])
                else:
                    stt("vector", ht[s][0:64, :], xt[sx][0:64, :], rstd[s][0:64, 0:1], gbc[0:64, :], ALU.mult, ALU.mult,
                        [kx, kr, "gbc"], [kh])
                    cp("gpsimd", ht[s][64:128, :], xt[sx][64:128, :], [kx, kh], [kh])

            def stage_b(i):
                s = i % 2
                sx = i % 4
                kx, kh, ks, kr = "xt%d" % sx, "ht%d" % s, "ss%d" % s, "rs%d" % s
                pa, ka = psum()
                pb, kb = psum()
                for j in range(8):
                    pp, kk_ = (pa, ka) if j < 4 else (pb, kb)
                    tr(pp[:, (j % 4) * 128:(j % 4 + 1) * 128], ht[s][:, j * 128:(j + 1) * 128], ident[:], [kh, "ident"], [kk_])
                pa3 = pa[:].rearrange("p (j c) -> p j c", c=128)
                pb3 = pb[:].rearrange("p (j c) -> p j c", c=128)
                if i < 16:
                    c0 = 1 + i * 128
                    cp("scalar", hT[:, 0:4, c0:c0 + 128], pa3, [ka], ["hT%d" % i])
                    cp("vector", hT[:, 4:8, c0:c0 + 128], pb3, [kb], ["hT%d" % i])
                    if i == 15:
                        dma(O["nsp"], ht[s][127:128, :], r=[kh], dkey="o_nsp")
                else:
                    cp("scalar", hT[:, 0:4, SC0:SC0 + 80], pa3[:, :, 0:80], [ka], ["hT16"])
                    cp("vector", hT[:, 4:8, SC0:SC0 + 80], pb3[:, :, 0:80], [kb], ["hT16"])
                    cp("scalar", ucb[:, 0:4, :], pa3[:, :, 80:112], [ka], ["ucb"])
                    cp("vector", ucb[:, 4:8, :], pb3[:, :, 80:112], [kb], ["ucb"])
                    dma(O["nss"], ht[s][3:64:4, :], r=[kh], dkey="o_nss")

            stage_a(0)
            for i in range(17):
                if i + 1 < 17:
                    stage_a(i + 1)
                stage_b(i)
            P.barrier()
        if dbg:
            DBG["hT"] = nc.dram_tensor("dbg_hT", [128, 8, HCOLS], BF16, kind="ExternalOutput").ap()
            dma(DBG["hT"], hT[:], dkey="dbg_hT")
            DBG["prm"] = nc.dram_tensor("dbg_prm", [128, 128], F32, kind="ExternalOutput").ap()
            dma(DBG["prm"], prm[:], dkey="dbg_prm")
            DBG["ucb"] = nc.dram_tensor("dbg_ucb", [128, 8, 32], F32, kind="ExternalOutput").ap()
            dma(DBG["ucb"], ucb[:], dkey="dbg_ucb")


        yr_d = nc.dram_tensor("yr_d", [128, 8, NTOK], BF16).ap()
        if stage >= 2:
          with ExitStack() as s2:
            sb2 = lambda name, shape, dt=F32: sb(name, shape, dt, st=s2)
            loraW = sb2("loraW", [128, 512]); loraA = sb2("loraA", [128, 512])
            mset("gpsimd", loraW[64:128, :], 0.0, ["lora"])
            mset("gpsimd", loraA[0:64, :], 0.0, ["lora"])
            pm = sb2("pm", [128, 2])
            mset("gpsimd", pm[:], 0.0, ["pm"])
            mset("gpsimd", pm[0:64, 0:1], 1.0, ["pm"])
            mset("gpsimd", pm[64:128, 1:2], 1.0, ["pm"])
            omu = sb2("omu", [128, 33])
            ts("vector", omu[:], prm[:, PR_MU:PR_MU + 33], -1.0, 1.0, ALU.mult, ALU.add, ["prm"], ["omu"])
            def dup(name, base, f=None):
                t = sb2(name, [128, 16])
                src = prm[:, base:base + 8].unsqueeze(2).to_broadcast([128, 8, 2])
                dst = t[:].rearrange("p (b c) -> p b c", c=2)
                if f is None:
                    cp("vector", dst, src, ["prm"], [name])
                else:
                    ts("vector", dst, src, f[0], f[1], ALU.mult, ALU.add, ["prm"], [name])
                return t
            kk2 = dup("kk2", PR_KK); ka2 = dup("ka2", PR_KA); omka2 = dup("omka2", PR_KA, (-1.0, 1.0))
            rk2 = dup("rk2", PR_RK); lnw2 = dup("lnw2", PR_LNW); lnb2 = dup("lnb2", PR_LNB)
            epsg = sb2("epsg", [128, 1]); mset("gpsimd", epsg[:], GN_EPS, ["epsg"])
            selT = sb2("selT", [8, 4, 128])
            mset("gpsimd", selT[:], 1.0, ["selT"])
            op("gpsimd", lambda e: e.affine_select(out=selT[:, :, 0:64], in_=selT[:, :, 0:64], pattern=[[-2, 4], [0, 64]],
                                                   compare_op=ALU.is_equal, fill=0.0, base=0, channel_multiplier=1), ["selT"], ["selT"])
            op("gpsimd", lambda e: e.affine_select(out=selT[:, :, 64:128], in_=selT[:, :, 64:128], pattern=[[-2, 4], [0, 64]],
                                                   compare_op=ALU.is_equal, fill=0.0, base=-1, channel_multiplier=1), ["selT"], ["selT"])
            rnt = [sb2("rnt%d" % i, [128, 8]) for i in range(2)]
            rnT = [sb2("rnT%d" % i, [8, 128]) for i in range(2)]
            bd4 = sb2("bd4", [128, 64])
            mset("gpsimd", bd4[:], 1.0, ["bd4"])
            def asel(ap, pattern, cmpop, base, cm, key):
                op("gpsimd", lambda e: e.affine_select(out=ap, in_=ap, pattern=pattern, compare_op=cmpop, fill=0.0,
                                                       base=base, channel_multiplier=cm), [key], [key])
            for half in range(2):
                v = bd4[half * 64:(half + 1) * 64, :].rearrange("p (b t) -> p b t", t=4)
                asel(v, [[-4, 16], [0, 4]], ALU.is_ge, 0, 1, "bd4")
                asel(v, [[4, 16], [0, 4]], ALU.is_ge, 3, -1, "bd4")
            maskT = [sb2("maskT%d" % i, [128, 128]) for i in range(2)]
            maskN = [sb2("maskN%d" % i, [64, 64]) for i in range(2)]
            Tcat = [sb2("Tcat%d" % i, [64, 128]) for i in range(2)]
            Tsuf = [sb2("Tsuf%d" % i, [64, 128]) for i in range(2)]
            mset("gpsimd", maskT[0][:], 1.0, ["mT"])
            for half in range(2):
                asel(maskT[0][half * 64:(half + 1) * 64, 0:64], [[1, 64]], ALU.is_gt, 0, -1, "mT")
                asel(maskT[0][half * 64:(half + 1) * 64, 64:128], [[1, 64]], ALU.is_ge, 0, -1, "mT")
            mset("gpsimd", maskN[0][:], 1.0, ["mN"])
            asel(maskN[0][:], [[-1, 64]], ALU.is_gt, 0, 1, "mN")
            mset("gpsimd", Tcat[0][:], -LD, ["Tc"])
            asel(Tcat[0][:, 0:64], [[1, 64]], ALU.is_ge, 0, -1, "Tc")
            asel(Tcat[0][:, 64:128], [[1, 64]], ALU.is_gt, 0, -1, "Tc")
            mset("gpsimd", Tsuf[0][:], -LD, ["Tsf"])
            asel(Tsuf[0][:, 0:64], [[-1, 64]], ALU.is_gt, 0, 1, "Tsf")
            asel(Tsuf[0][:, 64:128], [[-1, 64]], ALU.is_gt, 0, 1, "Tsf")
            for lo in (0, 64):
                tt("gpsimd", maskT[1][:, lo:lo + 64], maskT[0][:, lo:lo + 64], bd4[:], ALU.mult, ["mT", "bd4"], ["mTs"])
                tt("gpsimd", Tcat[1][:, lo:lo + 64], Tcat[0][:, lo:lo + 64], bd4[0:64, :], ALU.mult, ["Tc", "bd4"], ["Tcs"])
                tt("gpsimd", Tsuf[1][:, lo:lo + 64], Tsuf[0][:, lo:lo + 64], bd4[0:64, :], ALU.mult, ["Tsf", "bd4"], ["Tsfs"])
            tt("gpsimd", maskN[1][:], maskN[0][:], bd4[0:64, :], ALU.mult, ["mN", "bd4"], ["mNs"])
            cmask = sb2("cmask", [128, 16, 64])
            mset("gpsimd", cmask[:], 1.0, ["cmask"])
            asel(cmask[:], [[-4, 16], [1, 64]], ALU.is_ge, 0, 0, "cmask")
            asel(cmask[:], [[4, 16], [-1, 64]], ALU.is_ge, 3, 0, "cmask")
            idP = [sb2("idP%d" % i, [128, 64]) for i in range(2)]
            mset("gpsimd", idP[0][:], 0.0, ["idP"]); mset("gpsimd", idP[1][:], 0.0, ["idP"])
            cp("gpsimd", idP[0][0:64, :], ident[0:64, 0:64], ["ident", "idP"], ["idP"])
            cp("gpsimd", idP[1][64:128, :], ident[64:128, 64:128], ["ident", "idP"], ["idP"])
            P.barrier()

            T4 = lambda name: sb2(name, [128, 4, 128])
            CD = lambda name, shape, dt=F32: _DB([sb2("%s_c%d" % (name, i), shape, dt) for i in range(2)], P, "cpar")
            kT, aT, kkT, x1, gcen, gx1 = [T4(n) for n in ("kT", "aT", "kkT", "x1", "gcen", "gx1")]
            x2 = x1
            ysT = _DB([T4("ysT_%d" % i) for i in range(2)], P)
            DBN = ("rT", "zT", "nkk", "bonT")
            rT, zT, nkk, bonT = [_DB([T4("%s_%d" % (n, i)) for i in range(2)], P) for n in DBN]
            svS = _DB([sb2("svS_%d" % i, [128, 8, 128]) for i in range(2)], P)
            bkS = _DB([sb2("bkS_%d" % i, [128, 8, 128]) for i in range(2)], P)
            P.dbkeys = set(DBN) | {"svS", "bkS", "ysTh0", "ysTh1", "yrg"}
            P.cdbkeys = {"sig_tm", "UVv", "UVuh0", "UVuh1", "eI", "eNI", "eE", "esuf", "BKh", "ARa", "ARr", "BKt", "ATm"} | \
                {"%s%d%s" % (a_, i_, h_) for a_ in ("X", "XT", "TT") for i_ in range(2) for h_ in ("h0", "h1")}
            cen = gcen
            wa = sb2("wa", [128, 128])
            tmp1 = [sb2("tmp1_%d" % i, [128, 128]) for i in range(2)]
            sig_tm = CD("sig_tm", [64, 512])
            UV = CD("UV", [128, 8, 64], BF16); ATm = CD("ATm", [128, 8, 128], BF16)
            AR = CD("AR", [128, 4, 128], BF16)
            BKtP = [CD("BKtP%d" % i, [128, 4, 128], BF16) for i in range(2)]
            VP = CD("VP", [128, 8, 64], BF16)
            for c_ in range(2):
                mset("gpsimd", BKtP[0].tiles[c_][64:128], 0.0, ["cinit"]); mset("gpsimd", BKtP[1].tiles[c_][0:64], 0.0, ["cinit"])
                mset("gpsimd", VP.tiles[c_][0:64], 0.0, ["cinit"])
            yrg = _DB([sb2("yrg%d" % i, [128, 4, 128], BF16) for i in range(2)], P)
            HPd = [sb2("HPd%d" % i, [128, 4, 64], BF16) for i in range(2)]
            eI = CD("eI", [128, 4, 64]); eNI = CD("eNI", [128, 4, 64]); eE = CD("eE", [128, 4, 64])
            esuf = CD("esuf", [128, 512]); BKh = CD("BKh", [128, 512], BF16)
            Xb = [sb2("Xb%d" % i, [64, 8, 2, 64], BF16) for i in range(2)]
            XTb = [sb2("XTb%d" % i, [64, 8, 2, 64], BF16) for i in range(2)]
            TTb = [sb2("TTb%d" % i, [64, 8, 2, 64], BF16) for i in range(2)]
            Wsb = sb2("Wsb", [64, 8, 64], BF16)
            H = sb2("H", [128, 4, 64]); HP = sb2("HP", [128, 4, 64])
            H0T = arena[:, 0:4096].rearrange("p (b i v) -> p b i v", i=4, v=64)
            arena_b = arena[:, :].bitcast(BF16)
            ARm = [arena_b[:, 8192 + j * 1024:8192 + (j + 1) * 1024].rearrange("p (b s) -> p b s", s=64) for j in range(4)]
            H0Tb = arena_b[:, 12288:16384].rearrange("p (b i v) -> p b i v", i=4, v=64)
            h3 = lambda ap: ap.rearrange("p (h x) -> p h x", x=64)
            S0v = [h3(arena[0:64, 8192 + i * 512:8192 + (i + 1) * 512]) for i in range(2)]
            UVm = [h3(arena_b[:, 18432 + i * 1024:18432 + i * 1024 + 512]) for i in range(2)]
            Sout = [h3(arena[0:64, 10240 + i * 512:10240 + (i + 1) * 512]) for i in range(2)]
            cnt = {"t1": 0, "ev": 0}
            P.barrier()

            def v3(ap):
                return ap.rearrange("p (c s) -> p c s", s=64)

            def evac(out, in_, r, w):
                cnt["ev"] += 1
                cp("scalar" if cnt["ev"] % 2 else "vector", out, in_, r, w)

            for hp in range(2):
                nbl = pass_nbl(hp)
                if hp > 0:
                    for db in range(8):
                        load_wr(hp, db)
                dma(loraW[0:64, :], I["w_up"][:, 512 * hp:512 * hp + 512], r=["lora"], w=["lora"], dkey="lora")
                dma(loraA[64:128, :], I["a_up"][:, 512 * hp:512 * hp + 512], r=["lora"], w=["lora"], dkey="lora")
                mset("gpsimd", H[:], 0.0, ["Hh0", "Hh1"])
                mset("gpsimd", HPd[0][:], 0.0, ["HPdh0", "HPdh1"]); mset("gpsimd", HPd[1][:], 0.0, ["HPdh0", "HPdh1"])

                def front(g):
                    smp = g == 16
                    nt = 64 if smp else 128
                    nch = 1 if smp else 2
                    m = 1 if smp else 0
                    Q = 4 if smp else 8
                    p8 = slice(8 * hp, 8 * hp + 8, 2) if smp else slice(8 * hp, 8 * hp + 8)

                    def gv(t, smp=smp):
                        return t[:, :, 0:64] if smp else t[:, :, :].rearrange("p b (c s) -> p (b c) s", s=64)

                    def sv(S, lo, smp=smp):
                        return S[:, 0:8:2, lo:lo + 64] if smp else S[:, :, lo:lo + 64]

                    def pbc(t, p8=p8, Q=Q):
                        return t[:, p8].unsqueeze(2).to_broadcast([128, Q, 64])

                    for li, (col, mrow) in enumerate(nbl):
                        pb, pk = psum()
                        if smp:
                            rhs_cols = slice(SC0, SC0 + 80)
                            ncol = 80
                        else:
                            rhs_cols = slice(g * 128, g * 128 + 129)
                            ncol = 129
                        for db in range(8):
                            mm(pb[:, 0:ncol], wr[:, db, li * 128:(li + 1) * 128], hT[:, db, rhs_cols], db == 0, db == 7, ["wr"], [pk])
                        t1 = tmp1[cnt["t1"] % 2]; k1 = "tmp1_%d" % (cnt["t1"] % 2); cnt["t1"] += 1
                        cur = pb[:, 0:64] if smp else pb[:, 1:129]
                        op("scalar", lambda e, t1=t1, cur=cur, mrow=mrow, nt=nt: e.mul(out=t1[:, 0:nt], in_=cur, mul=omu[:, mrow:mrow + 1]),
                           [pk, "omu"], [k1])
                        if li < 4:
                            dst, dk = rT[:, li, 0:nt], "rT"
                        elif li < 8:
                            dst, dk = kT[:, li - 4, 0:nt], "kT"
                        elif li < 12:
                            dst, dk = None, "svS"
                        elif li == 12:
                            dst, dk = wa[:, 0:nt], "wa"
                        else:
                            dst, dk = zT[:, li - 13, 0:nt], "zT"
                        musc = prm[:, PR_MU + mrow:PR_MU + mrow + 1]
                        if not smp:
                            if dst is None:
                                bi = li - 8
                                stt("vector", svS[:, 2 * bi:2 * bi + 2, 64:128], v3(pb[:, 0:128]), musc, v3(t1[:, 0:128]),
                                    ALU.mult, ALU.add, [pk, k1, "prm"], [dk])
                            else:
                                stt("vector", dst, pb[:, 0:128], musc, t1[:, 0:128], ALU.mult, ALU.add, [pk, k1, "prm"], [dk])
                        else:
                            if dst is None:
                                dst = svS[:, 2 * (li - 8), 64:128]
                            d3 = dst.rearrange("p (b t) -> p b t", t=4)
                            c3 = pb[:, 0:64].rearrange("p (b t) -> p b t", t=4)
                            t3 = t1[:, 0:64].rearrange("p (b t) -> p b t", t=4)
                            stt("vector", d3[:, :, 1:4], c3[:, :, 0:3], musc, t3[:, :, 1:4], ALU.mult, ALU.add, [pk, k1, "prm"], [dk])
                            stt("vector", d3[:, :, 0:1], pb[:, 64:80].unsqueeze(2), musc, t3[:, :, 0:1], ALU.mult, ALU.add,
                                [pk, k1, "prm"], [dk])
                        yield

                    act(wa[0:64, 0:nt], wa[0:64, 0:nt], AF.Tanh, ["wa"], ["wa"])
                    yield
                    for bi in range(4):
                        blk = 4 * hp + bi
                        pb, pk = psum()
                        mm(pb[:, 0:nt], loraW[:, bi * 128:(bi + 1) * 128], wa[:, 0:nt], True, True, ["lora", "wa"], [pk])
                        mm(pb[:, 128:128 + nt], loraA[:, bi * 128:(bi + 1) * 128], wa[:, 0:nt], True, True, ["lora", "wa"], [pk])
                        so = svS[:, 2 * bi, 0:64] if smp else svS[:, 2 * bi:2 * bi + 2, 0:64]
                        si = pb[:, 0:64] if smp else v3(pb[:, 0:128])
                        act(so, si, AF.Sigmoid, [pk, "prm"], ["svS"], bias=prm[:, PR_W0 + blk:PR_W0 + blk + 1], scale=1.0)
                        act(aT[:, bi, 0:nt], pb[:, 128:128 + nt], AF.Sigmoid, [pk, "prm"], ["aT"],
                            bias=prm[:, PR_A0 + blk:PR_A0 + blk + 1], scale=1.0)
                        yield
                    tt("gpsimd", gv(kkT), gv(kT), pbc(kk2), ALU.mult, ["kT", "kk2"], ["kkT"])
                    tt("gpsimd", gv(x1), gv(kkT), gv(kkT), ALU.mult, ["kkT"], ["x1"])
                    yield
                    pc, kc = psum()
                    for bi in range(4):
                        mm(pc[0:nt, 2 * bi:2 * bi + 2], x1[:, bi, 0:nt], pm[:, 0:2], True, True, ["x1", "pm"], [kc])
                    act(rnt[0][0:nt, :], pc[0:nt, 0:8], AF.Sqrt, [kc], ["rnt0"])
                    yield
                    ts1("vector", rnt[0][0:nt, :], rnt[0][0:nt, :], 1e-12, ALU.max, ["rnt0"], ["rnt0"])
                    op("vector", lambda e, nt=nt: e.reciprocal(out=rnt[0][0:nt, :], in_=rnt[0][0:nt, :]), ["rnt0"], ["rnt0"])
                    pt, kt = psum()
                    tr(pt[0:8, 0:nt], rnt[0][0:nt, :], ident[0:nt, 0:nt], ["rnt0", "ident"], [kt])
                    cp("scalar", rnT[0][:, 0:nt], pt[0:8, 0:nt], [kt], ["rnT0"])
                    yield
                    pb, pk = psum()
                    for bi in range(4):
                        mm(pb[:, bi * nt:(bi + 1) * nt], selT[:, bi, :], rnT[0][:, 0:nt], True, True, ["selT", "rnT0"], [pk])
                    pbn = pb[:, 0:256].rearrange("p (b s) -> p b s", s=64) if smp else pb[:, 0:512].rearrange("p (q s) -> p q s", s=64)
                    stt("vector", gv(nkk), gv(kkT), -1.0, pbn, ALU.mult, ALU.mult, ["kkT", pk], ["nkk"])
                    yield
                    tt("gpsimd", gv(x2), gv(aT), pbc(ka2), ALU.mult, ["aT", "ka2"], ["x1"])
                    tt("gpsimd", gv(x2), gv(x2), pbc(omka2), ALU.add, ["x1", "omka2"], ["x1"])
                    tt("gpsimd", sv(bkS, 64), gv(kT), gv(x2), ALU.mult, ["kT", "x1"], ["bkS"])
                    yield
                    stt("vector", sv(bkS, 0), gv(nkk), -1.0, gv(aT), ALU.mult, ALU.mult, ["nkk", "aT"], ["bkS"])
                    yield
                    tt("gpsimd", gv(x2), gv(rT), sv(bkS, 64), ALU.mult, ["rT", "bkS"], ["x1"])
                    tt("gpsimd", gv(x2), gv(x2), pbc(rk2), ALU.mult, ["x1", "rk2"], ["x1"])
                    yield
                    pb, pk = psum()
                    mm(pb[:, 0:4 * nt].rearrange("p (b t) -> p b t", t=nt), bones[:], x2[:, :, 0:nt], True, True, ["bones", "x1"], [pk])
                    pbv = pb[:, 0:256].rearrange("p (b s) -> p b s", s=64) if smp else pb[:, 0:512].rearrange("p (q s) -> p q s", s=64)
                    tt("vector", gv(bonT), pbv, sv(svS, 64), ALU.mult, [pk, "svS"], ["bonT"])
                    yield
                    act(zT[:, :, 0:nt], zT[:, :, 0:nt], AF.Silu, ["zT"], ["zT"])


                def gnorm(g):
                    smp = g == 16
                    nt = 64 if smp else 128
                    nch = 1 if smp else 2
                    m = 1 if smp else 0
                    Q = 4 if smp else 8
                    p8 = slice(8 * hp, 8 * hp + 8, 2) if smp else slice(8 * hp, 8 * hp + 8)

                    def gv(t, smp=smp):
                        return t[:, :, 0:64] if smp else t[:, :, :].rearrange("p b (c s) -> p (b c) s", s=64)

                    def sv(S, lo, smp=smp):
                        return S[:, 0:8:2, lo:lo + 64] if smp else S[:, :, lo:lo + 64]

                    def pbc(t, p8=p8, Q=Q):
                        return t[:, p8].unsqueeze(2).to_broadcast([128, Q, 64])

                    ntq = 4 * nt
                    r3 = lambda p_: p_[:, 0:ntq].rearrange("p (b t) -> p b t", t=nt)
                    pb, pk = psum()
                    mm(r3(pb), bones[:], ysT[:, :, 0:nt], True, True, ["bones", "ysTh0", "ysTh1"], [pk])
                    stt("vector", cen[:, :, 0:nt], r3(pb), -1.0 / 64, ysT[:, :, 0:nt], ALU.mult, ALU.add, [pk, "ysTh0", "ysTh1"], ["gcen"])
                    yield
                    tt("gpsimd", gx1[:, :, 0:nt], cen[:, :, 0:nt], cen[:, :, 0:nt], ALU.mult, ["gcen"], ["gx1"])
                    yield
                    pc, kc = psum()
                    for bi in range(4):
                        mm(pc[0:nt, 2 * bi:2 * bi + 2], gx1[:, bi, 0:nt], pm[:, 0:2], True, True, ["gx1", "pm"], [kc])
                    act(rnt[1][0:nt, :], pc[0:nt, 0:8], AF.Sqrt, [kc, "epsg"], ["rnt1"], bias=epsg[0:nt, 0:1], scale=1.0 / 64)
                    yield
                    op("vector", lambda e, nt=nt: e.reciprocal(out=rnt[1][0:nt, :], in_=rnt[1][0:nt, :]), ["rnt1"], ["rnt1"])
                    pt, kt = psum()
                    tr(pt[0:8, 0:nt], rnt[1][0:nt, :], ident[0:nt, 0:nt], ["rnt1", "ident"], [kt])
                    cp("scalar", rnT[1][:, 0:nt], pt[0:8, 0:nt], [kt], ["rnT1"])
                    yield
                    pb, pk = psum()
                    for bi in range(4):
                        mm(pb[:, bi * nt:(bi + 1) * nt], selT[:, bi, :], rnT[1][:, 0:nt], True, True, ["selT", "rnT1"], [pk])
                    pbn = pb[:, 0:256].rearrange("p (b s) -> p b s", s=64) if smp else pb[:, 0:512].rearrange("p (q s) -> p q s", s=64)
                    tt("vector", gv(cen), gv(cen), pbn, ALU.mult, ["gcen", pk], ["gcen"])
                    yield
                    tt("gpsimd", gv(cen), gv(cen), pbc(lnw2), ALU.mult, ["gcen", "lnw2"], ["gcen"])
                    yield
                    tt("gpsimd", gv(cen), gv(cen), pbc(lnb2), ALU.add, ["gcen", "lnb2"], ["gcen"])
                    yield
                    tt("gpsimd", gv(cen), gv(cen), gv(bonT), ALU.add, ["gcen", "bonT"], ["gcen"])
                    yield
                    oc = slice(SEQ, SEQ + 64) if smp else slice(g * 128, g * 128 + 128)
                    tt("vector", yrg[:, :, 0:nt], cen[:, :, 0:nt], zT[:, :, 0:nt], ALU.mult, ["gcen", "zT"], ["yrg"])
                    dma(yr_d[:, 4 * hp:4 * hp + 4, oc], yrg[:, :, 0:nt], r=["yrg"], dkey="yrg%d" % P.par)
                    yield

                def drain(gen, par):
                    old = P.par
                    P.par = par
                    for _ in gen:
                        pass
                    P.par = old

                st8 = {"gn": iter(()), "gnpar": 0}
                drain(front(0), 0)
                for g in range(17):
                    P.par = g % 2
                    nxt = front(g + 1) if g < 16 else iter(())

                    fcnt = [0]

                    def pump(k, pref="gn", g=g, nxt=nxt, fcnt=fcnt):
                        old = P.par
                        for _ in range(k):
                            order = ("front", "gn") if (pref == "front" and fcnt[0] < 13) else ("gn", "front")
                            for q in order:
                                if q == "gn":
                                    P.par = st8["gnpar"]
                                    try:
                                        next(st8["gn"])
                                        break
                                    except StopIteration:
                                        continue
                                else:
                                    P.par = (g + 1) % 2
                                    try:
                                        next(nxt)
                                        fcnt[0] += 1
                                        break
                                    except StopIteration:
                                        continue
                        P.par = old
                    lim = 99
                    smp = g == 16
                    nt = 64 if smp else 128
                    nch = 1 if smp else 2
                    m = 1 if smp else 0
                    Q = 4 if smp else 8
                    p8 = slice(8 * hp, 8 * hp + 8, 2) if smp else slice(8 * hp, 8 * hp + 8)

                    def gv(t, smp=smp):
                        return t[:, :, 0:64] if smp else t[:, :, :].rearrange("p b (c s) -> p (b c) s", s=64)

                    def sv(S, lo, smp=smp):
                        return S[:, 0:8:2, lo:lo + 64] if smp else S[:, :, lo:lo + 64]

                    def pbc(t, p8=p8, Q=Q):
                        return t[:, p8].unsqueeze(2).to_broadcast([128, Q, 64])

                    if smp:
                        P.barrier()
                        pb, pk = psum()
                        for bi in range(4):
                            tr(pb[0:64, bi * 128:(bi + 1) * 128], H[:, bi, :], ident[:], ["Hh0", "Hh1", "ident"], [pk])
                        cp("vector", Sout[0][:], pb[0:64, :].rearrange("p (h k) -> p h k", k=64), [pk], ["Sout0"])
                        dma(O["nrp"][8 * hp:8 * hp + 8, :, :].rearrange("h v k -> v h k"), Sout[0][:], r=["Sout0"], dkey="o_nrs0")
                        for b in range(16):
                            s = b % 2
                            dma(S0v[s][:], I["srwkv"][b, 8 * hp:8 * hp + 8, :, :].rearrange("h v k -> v h k"), w=["S0v%d" % s],
                                dkey="S0v%d" % s)
                            pb, pk = psum()
                            for bi in range(4):
                                tr(pb[:, bi * 64:(bi + 1) * 64], S0v[s][:, 2 * bi:2 * bi + 2, :].rearrange("p a k -> p (a k)"),
                                   ident[0:64, 0:64], ["S0v%d" % s, "ident"], [pk])
                            evac(H0T[:, b, :, :], pb[:, 0:256].rearrange("p (b v) -> p b v", v=64), [pk], ["H0T"])
                        cp("vector", arena_b[:, 12288:16384], arena[:, 0:4096], ["H0T"], ["H0Tb"])

                    def prepA(ch):
                        cs = slice(ch * 64, (ch + 1) * 64)
                        pX, kX = psum()
                        for bi in range(4):
                            tr(pX[:, bi * 128:(bi + 1) * 128], svS[:, 2 * bi + ch, :], ident[:], ["svS", "ident"], [kX])
                        cp("scalar", sig_tm[:, :], pX[0:64, :], [kX], ["sig_tm"])
                        cp("vector", UV[64:128, :, :], pX[64:128, :].rearrange("p (h v) -> p h v", v=64), [kX], ["UVv"])
                        cp("scalar", VP[64:128, :, :], pX[64:128, :].rearrange("p (h v) -> p h v", v=64), [kX], ["UVv"])
                        yield
                        pC, kC = psum()
                        for bi in range(4):
                            mm(pC[:, bi * 128:(bi + 1) * 128], sig_tm[:, bi * 128:(bi + 1) * 128], Tcat[m][:], True, True,
                               ["sig_tm", "Tc", "Tcs"], [kC])
                        pC3 = pC[:].rearrange("p (b x) -> p b x", x=128)
                        act(eI[:], pC3[:, :, 0:64], AF.Exp, [kC], ["eI"])
                        act(eNI[:], pC3[:, :, 0:64], AF.Exp, [kC], ["eNI"], scale=-1.0)
                        act(eE[:], pC3[:, :, 64:128], AF.Exp, [kC], ["eE"])
                        yield
                        pS, kS = psum()
                        mm(pS[:], Tsuf[m][:], sig_tm[:], True, True, ["sig_tm", "Tsf", "Tsfs"], [kS])
                        act(esuf[:], pS[:], AF.Exp, [kS], ["esuf"])
                        pY, kY = psum()
                        for bi in range(4):
                            tr(pY[:, bi * 128:(bi + 1) * 128], bkS[:, 2 * bi + ch, :], ident[:], ["bkS", "ident"], [kY])
                        tt("vector", BKh[:], pY[:], esuf[:], ALU.mult, [kY, "esuf"], ["BKh"])
                        yield
                        tt("gpsimd", AR[:, :, 0:64], nkk[:, :, cs], eE[:], ALU.mult, ["nkk", "eE"], ["ARa"])
                        tt("gpsimd", AR[:, :, 64:128], rT[:, :, cs], eI[:], ALU.mult, ["rT", "eI"], ["ARr"])
                        for par in range(2):
                            ps_ = slice(par * 64, par * 64 + 64)
                            tt("vector", BKtP[par][ps_, :, 0:64], bkS[ps_, ch:8:2, 0:64], eNI[ps_], ALU.mult, ["bkS", "eNI"], ["BKt"])
                            tt("vector", BKtP[par][ps_, :, 64:128], bkS[ps_, ch:8:2, 64:128], eNI[ps_], ALU.mult, ["bkS", "eNI"], ["BKt"])
                        yield
                        for q in range(2):
                            pA, kA = psum()
                            for j in range(4):
                                hl = 4 * q + j
                                bi, par = hl // 2, hl % 2
                                ps_ = slice(par * 64, par * 64 + 64)
                                mm(pA[:, j * 128:(j + 1) * 128], BKtP[par][:, bi, :], AR[:, bi, :], True, True, ["BKt", "ARa", "ARr"], [kA])
                            tt("vector", ATm[:, 4 * q:4 * q + 4, :], pA[:].rearrange("p (h x) -> p h x", x=128),
                               maskT[m][:].unsqueeze(1).to_broadcast([128, 4, 128]), ALU.mult, [kA, "mT", "mTs"], ["ATm"])
                            yield
                        pN, kN = psum()
                        for hl in range(8):
                            bi, par = hl // 2, hl % 2
                            ps_ = slice(par * 64, par * 64 + 64)
                            mm(pN[0:64, hl * 64:(hl + 1) * 64], AR[:, bi, 0:64], BKtP[par][:, bi, 0:64], True, True, ["ARa", "BKt"], [kN])
                        tt("vector", Xb[0][:, :, ch, :], pN[0:64, :].rearrange("p (h x) -> p h x", x=64),
                           maskN[m][:].unsqueeze(1).to_broadcast([64, 8, 64]), ALU.mult, [kN, "mN", "mNs"], ["X0h0", "X0h1"])
                        tt("vector", TTb[0][:, :, ch, :], ATm[0:64, :, 0:64], ident[0:64, 0:64].unsqueeze(1).to_broadcast([64, 8, 64]), ALU.add,
                           ["ATm", "ident"], ["TT0h0", "TT0h1"])
                        cp("scalar", XTb[1][:, :, ch, :], ATm[0:64, :, 0:64], ["ATm"], ["XT1h0", "XT1h1"])

                    def powers():
                        Xc, XTc, TTc = Xb[0], XTb[1], TTb[0]
                        kXc, kXTc, kTTc = "X0", "XT1", "TT0"
                        ck = lambda base: [base + "#%d" % c_ for c_ in range(nch)]
                        for lvl in range(5):
                            Xn = Xb[(lvl + 1) % 2]; kXn = "X%d" % ((lvl + 1) % 2)
                            TTn = TTb[(lvl + 1) % 2]; kTTn = "TT%d" % ((lvl + 1) % 2)
                            if lvl < 4:
                                XTn = XTb[lvl % 2]; kXTn = "XT%d" % (lvl % 2)
                            for hh in range(2):
                                hs = slice(4 * hh, 4 * hh + 4)
                                sfx = "h%d" % hh
                                pv = lambda p_: p_[0:64, 0:256 * nch].rearrange("p (h c x) -> p h c x", c=nch, x=64)
                                col = lambda hl, c_: slice(((hl % 4) * nch + c_) * 64, ((hl % 4) * nch + c_ + 1) * 64)
                                p1, k1_ = psum()
                                for c_ in range(nch):
                                    for hl in range(4 * hh, 4 * hh + 4):
                                        mm(p1[0:64, col(hl, c_)], XTc[:, hl, c_, :], Xc[:, hl, c_, :], True, True,
                                           [kXTc + sfx + "#%d" % c_, kXc + sfx + "#%d" % c_], [k1_])
                                cp("scalar", Xn[:, hs, 0:nch, :], pv(p1), [k1_], ck(kXn + sfx))
                                if lvl < 4:
                                    p2, k2_ = psum()
                                    for c_ in range(nch):
                                        for hl in range(4 * hh, 4 * hh + 4):
                                            mm(p2[0:64, col(hl, c_)], Xc[:, hl, c_, :], XTc[:, hl, c_, :], True, True,
                                               [kXTc + sfx + "#%d" % c_, kXc + sfx + "#%d" % c_], [k2_])
                                    cp("vector", XTn[:, hs, 0:nch, :], pv(p2), [k2_], ck(kXTn + sfx))
                            yield
                            for hh in range(2):
                                hs = slice(4 * hh, 4 * hh + 4)
                                sfx = "h%d" % hh
                                pv = lambda p_: p_[0:64, 0:256 * nch].rearrange("p (h c x) -> p h c x", c=nch, x=64)
                                col = lambda hl, c_: slice(((hl % 4) * nch + c_) * 64, ((hl % 4) * nch + c_ + 1) * 64)
                                p3, k3_ = psum()
                                for c_ in range(nch):
                                    for hl in range(4 * hh, 4 * hh + 4):
                                        mm(p3[0:64, col(hl, c_)], Xn[:, hl, c_, :], TTc[:, hl, c_, :], True, True,
                                           [kXn + sfx + "#%d" % c_, kTTc + sfx + "#%d" % c_], [k3_])
                                tt("vector", TTn[:, hs, 0:nch, :], pv(p3), TTc[:, hs, 0:nch, :], ALU.add,
                                   [k3_] + ck(kTTc + sfx), ck(kTTn + sfx))
                            yield
                            Xc, kXc = Xn, kXn
                            if lvl < 4:
                                XTc, kXTc = XTn, kXTn
                            TTc, kTTc = TTn, kTTn

                    def stateB(ch):
                        cs = slice(ch * 64, (ch + 1) * 64)
                        if not smp:
                            for hh in range(2):
                                bs = slice(2 * hh, 2 * hh + 2)
                                tt("gpsimd", HP[:, bs], H[:, bs], eI[:, bs, 63:64].to_broadcast([128, 2, 64]), ALU.mult,
                                   ["Hh%d" % hh, "eI"], ["HPh%d" % hh])
                        TTc, kTTc = TTb[1][:, :, ch, :], "TT1"
                        def st_W(hh):
                            sfx = "h%d" % hh
                            pW, kW = psum()
                            for hl in range(4 * hh, 4 * hh + 4):
                                bi, par = hl // 2, hl % 2
                                o_ = pW[0:64, (hl % 4) * 64:(hl % 4 + 1) * 64]
                                if smp:
                                    ma, kma = ARm[hl % 2], "ARm%d" % (hl % 2)
                                    stt("vector", ma[:], AR[:, bi, 0:64].unsqueeze(1).to_broadcast([128, 16, 64]), pm[:, par:par + 1], cmask[:],
                                        ALU.mult, ALU.mult, ["ARa", "cmask", "pm"], [kma])
                                    for b in range(16):
                                        mm(o_, ma[:, b, :], H0Tb[:, b, bi, :], b == 0, False, [kma, "H0Tb"], [kW])
                                else:
                                    mm(o_, AR[:, bi, 0:64], HPd[par][:, bi, :], True, False, ["ARa", "HPd" + sfx], [kW])
                                mm(o_, ATm[:, hl, 0:64], VP[:, hl, :], False, True, ["ATm", "UVv"], [kW])
                            cp("scalar", Wsb[:, 4 * hh:4 * hh + 4, :], pW[0:64, 0:256].rearrange("p (h x) -> p h x", x=64), [kW], ["Wsb" + sfx])

                        def st_U(hh):
                            sfx = "h%d" % hh
                            pU, kU = psum()
                            for hl in range(4 * hh, 4 * hh + 4):
                                mm(pU[0:64, (hl % 4) * 64:(hl % 4 + 1) * 64], TTc[:, hl, :], Wsb[:, hl, :], True, True, [kTTc + sfx, "Wsb" + sfx], [kU])
                            cp("vector", UV[0:64, 4 * hh:4 * hh + 4, :], pU[0:64, 0:256].rearrange("p (h x) -> p h x", x=64), [kU], ["UVu" + sfx])

                        def st_Y(hh):
                            sfx = "h%d" % hh
                            pYs, kYs = psum()
                            for hl in range(4 * hh, 4 * hh + 4):
                                bi, par = hl // 2, hl % 2
                                ps_ = slice(par * 64, par * 64 + 64)
                                o_ = pYs[ps_, (bi % 2) * 64:(bi % 2 + 1) * 64]
                                if smp:
                                    mr, kmr = ARm[2 + hl % 2], "ARm%d" % (2 + hl % 2)
                                    stt("vector", mr[:], AR[:, bi, 64:128].unsqueeze(1).to_broadcast([128, 16, 64]), pm[:, par:par + 1], cmask[:],
                                        ALU.mult, ALU.mult, ["ARr", "cmask", "pm"], [kmr])
                                    for b in range(16):
                                        mm(o_, H0Tb[:, b, bi, :], mr[:, b, :], b == 0, False, [kmr, "H0Tb"], [kYs])
                                else:
                                    mm(o_, HPd[par][:, bi, :], AR[:, bi, 64:128], True, False, ["ARr", "HPd" + sfx], [kYs])
                                mm(o_, UV[:, hl, :], ATm[:, hl, 64:128], False, True, ["UVu" + sfx, "UVv", "ATm"], [kYs])
                            cp("scalar", ysT[:, 2 * hh:2 * hh + 2, cs], pYs[:, 0:128].rearrange("p (b s) -> p b s", s=64), [kYs], ["ysT" + sfx])

                        def st_H(hh):
                            sfx = "h%d" % hh
                            bs = slice(2 * hh, 2 * hh + 2)
                            pH, kH = psum()
                            for hl in range(4 * hh, 4 * hh + 4):
                                bi, par = hl // 2, hl % 2
                                ps_ = slice(par * 64, par * 64 + 64)
                                mm(pH[ps_, (bi % 2) * 64:(bi % 2 + 1) * 64], BKh[:, hl * 64:(hl + 1) * 64], UV[:, hl, :], True, True,
                                   ["BKh", "UVu" + sfx, "UVv"], [kH])
                            pH3 = pH[:, 0:128].rearrange("p (b v) -> p b v", v=64)
                            tt("vector", HPd[0][0:64, bs], pH3[0:64], HP[0:64, bs], ALU.add, [kH, "HP" + sfx], ["HPd" + sfx])
                            tt("vector", HPd[1][64:128, bs], pH3[64:128], HP[64:128, bs], ALU.add, [kH, "HP" + sfx], ["HPd" + sfx])
                            tt("vector", H[:, bs], pH3, HP[:, bs], ALU.add, [kH, "HP" + sfx], ["H" + sfx])

                        st_W(0); pump(1); st_W(1); pump(1); st_U(0); pump(1); st_U(1); pump(1)
                        if not smp:
                            st_Y(0); st_H(0); pump(1); st_Y(1); st_H(1); pump(1)
                        else:
                            st_Y(0); st_Y(1)
                        if smp:
                            for bi in range(4):
                                tt("gpsimd", H0T[:, :, bi, :], H0T[:, :, bi, :], eI[:, bi, 3:64:4].unsqueeze(2).to_broadcast([128, 16, 64]),
                                   ALU.mult, ["H0T", "eI"], ["H0T"])
                            for b in range(16):
                                s = b % 2
                                op("scalar", lambda e, s=s, b=b: e.mul(out=UVm[s][:], in_=UV[:], mul=bd4[:, 4 * b:4 * b + 1]),
                                   ["UVuh0", "UVuh1", "UVv", "bd4"], ["UVm%d" % s])
                                pF, kF = psum()
                                pG, kG = psum()
                                for hl in range(8):
                                    bi, par = hl // 2, hl % 2
                                    mm(pF[0:64, hl * 64:(hl + 1) * 64], H0T[:, b, bi, :], idP[par][:], True, True, ["H0T", "idP"], [kF])
                                for hl in range(8):
                                    mm(pG[0:64, hl * 64:(hl + 1) * 64], UVm[s][:, hl, :], BKh[:, hl * 64:(hl + 1) * 64], True, True,
                                       ["UVm%d" % s, "BKh"], [kG])
                                cp("scalar", Sout[s][:], pF[0:64, :].rearrange("p (h k) -> p h k", k=64), [kF], ["Sout%d" % s])
                                tt("vector", Sout[s][:], pG[0:64, :].rearrange("p (h k) -> p h k", k=64), Sout[s][:], ALU.add,
                                   [kG, "Sout%d" % s], ["Sout%d" % s])
                                dma(O["nrs"][b, 8 * hp:8 * hp + 8, :, :].rearrange("h v k -> v h k"), Sout[s][:], r=["Sout%d" % s],
                                    dkey="o_nrs%d" % s)

                    gens = [prepA(c_) for c_ in range(nch)]
                    alive = [True] * nch
                    while any(alive):
                        for c_ in range(nch):
                            if alive[c_]:
                                P.cpar = c_
                                try:
                                    next(gens[c_])
                                except StopIteration:
                                    alive[c_] = False
                        pump(1, "front")
                    for _ in powers():
                        pump(1, "gn")
                    for c_ in range(nch):
                        if c_ == nch - 1:
                            pump(1000)
                        P.cpar = c_
                        stateB(c_)
                    pump(1000)
                    st8["gn"], st8["gnpar"] = gnorm(g), g % 2
                drain(st8["gn"], st8["gnpar"])
                P.barrier()
        sAr.close()
        if stage >= 3:
          with ExitStack() as sA:
            sbA = lambda name, shape, dt=F32: sb(name, shape, dt, st=sA)
            mT = sbA("mT", [128, 8, NTOK], BF16)
            TCH = [(1 + 512 * q, 512 * q, 512) for q in range(4)] + [(SC0, SEQ, 64)]
            with ExitStack() as sB:
                sbB = lambda name, shape, dt=F32: sb(name, shape, dt, st=sB)
                ycT = sbB("ycT", [128, 8, NTOK], BF16)
                yrT = sbB("yrT", [128, 8, NTOK], BF16)
                wcs = [sbB("wcs%d" % i, [128, 8, 256]) for i in range(2)]
                wcb = [sbB("wcb%d" % i, [128, 8, 512], BF16) for i in range(2)]
                wcount = {"n": 0}

                def load4(srcs):
                    i = wcount["n"] % 2
                    wcount["n"] += 1
                    for h in range(2):
                        sl = (2 * wcount["n"] + h) % 2
                        kw = "wcs%d" % sl
                        for j in range(2):
                            dma(wcs[sl][:, :, j * 128:(j + 1) * 128], srcs[2 * h + j].rearrange("(d p) c -> p d c", p=128), w=[kw], dkey=kw,
                                eng="sync" if j == 0 else "gpsimd")
                        cp("scalar" if h == 0 else "vector", wcb[i][:, :, h * 256:(h + 1) * 256], wcs[sl][:], [kw], ["wcb%d" % i])
                    return wcb[i], "wcb%d" % i

                with ExitStack() as sC:
                    sbC = lambda name, shape, dt=F32: sb(name, shape, dt, st=sC)
                    ub = sbC("ub", [128, 2 + SEQ])
                    us = sbC("us", [128, 16, 6])
                    tmpx = sbC("tmpx", [128, 512]); acc = sbC("acc", [128, 512]); sz = sbC("sz", [128, 512]); t2 = sbC("t2", [128, 512])
                    ncst = sbC("ncst", [128, 8, 34]); ncrow = sbC("ncrow", [34, D])
                    conv_srcs = lambda cb: [I["w_in"][:, base + cb * 128:base + (cb + 1) * 128] for base in (CB_XIN, CB_B, CB_C, CB_ZC)]
                    tail_srcs = lambda db: [I["w_out_c"][:, db * 128:(db + 1) * 128], I["w_out_r"][:, db * 128:(db + 1) * 128],
                                            I["w_in"][:, CB_GC + db * 128:CB_GC + (db + 1) * 128],
                                            I["w_in"][:, CB_GR + db * 128:CB_GR + (db + 1) * 128]]
                    wnext = load4(conv_srcs(0))
                    dma(yrT[:], yr_d, w=["yrT"], dkey="yrT")
                    for cb in range(8):
                        wt, kwt = wnext
                        mset("gpsimd", ub[:, 0:2], 0.0, ["ub"])
                        cp("gpsimd", us[:, :, 0:2], ucb[:, cb, :].rearrange("p (b j) -> p b j", j=2), ["us"], ["us"])
                        cw = lambda j: prm[:, PR_CW + 8 * j + cb:PR_CW + 8 * j + cb + 1]
                        for ci_, (hc, oc, n) in enumerate(TCH):
                            if ci_ == 1:
                                wnext = load4(conv_srcs(cb + 1)) if cb < 7 else (load4(tail_srcs(0)) if stage >= 4 else None)
                            smp = n == 64
                            pp_ = [psum() for _ in range(4)]
                            for gi in range(4):
                                for db in range(8):
                                    mm(pp_[gi][0][:, 0:n], wt[:, db, gi * 128:(gi + 1) * 128], hT[:, db, hc:hc + n], db == 0, db == 7, [kwt], [pp_[gi][1]])
                            (pxin, kxin), (pbg, kbg), (pcg, kcg), (pzc, kzc) = pp_
                            cp("scalar", tmpx[:, 0:n], pxin[:, 0:n], [kxin], ["tmpx"])
                            if not smp:
                                tt("vector", ub[:, 2 + oc:2 + oc + n], pcg[:, 0:n], tmpx[:, 0:n], ALU.mult, [kcg, "tmpx", "ub"], ["ub"])
                                w_ = lambda j: ub[:, oc + j:oc + j + n]
                                a_ = acc[:, 0:n]
                            else:
                                tt("vector", us[:, :, 2:6], pcg[:, 0:64].rearrange("p (b t) -> p b t", t=4),
                                   tmpx[:, 0:64].rearrange("p (b t) -> p b t", t=4), ALU.mult, [kcg, "tmpx", "us"], ["us"])
                                w_ = lambda j: us[:, :, j:j + 4]
                                a_ = acc[:, 0:64].rearrange("p (b t) -> p b t", t=4)
                            ku = "us" if smp else "ub"
                            op("scalar", lambda e, a_=a_, w2=w_(2), c2=cw(2): e.mul(out=a_, in_=w2, mul=c2), [ku, "prm"], ["acc"])
                            stt("vector", a_, w_(1), cw(1), a_, ALU.mult, ALU.add, [ku, "acc", "prm"], ["acc"])
                            stt("vector", a_, w_(0), cw(0), a_, ALU.mult, ALU.add, [ku, "acc", "prm"], ["acc"])
                            act(sz[:, 0:n], pzc[:, 0:n], AF.Silu, [kzc], ["sz"])
                            tt("vector", t2[:, 0:n], pbg[:, 0:n], acc[:, 0:n], ALU.mult, [kbg, "acc"], ["t2"])
                            tt("gpsimd", ycT[:, cb, oc:oc + n], t2[:, 0:n], sz[:, 0:n], ALU.mult, ["t2", "sz"], ["ycT"])
                        cp("gpsimd", ncst[:, cb, 0:32].rearrange("p (b j) -> p b j", j=2), us[:, :, 4:6], ["us", "ncst"], ["ncst"])
                        cp("gpsimd", ncst[:, cb, 32:34], ub[:, SEQ:SEQ + 2], ["ub", "ncst"], ["ncst"])
                    pa, ka = psum()
                    pb, kb = psum()
                    for cb in range(8):
                        pp, kk_ = (pa, ka) if cb < 4 else (pb, kb)
                        tr(pp[0:34, (cb % 4) * 128:(cb % 4 + 1) * 128], ncst[:, cb, :], ident[:], ["ncst", "ident"], [kk_])
                    cp("scalar", ncrow[:, 0:512], pa[0:34, :], [ka], ["ncrow"])
                    cp("vector", ncrow[:, 512:1024], pb[0:34, :], [kb], ["ncrow"])
                    dma(O["ncs"], ncrow[0:32, :], r=["ncrow"], dkey="o_ncs")
                    dma(O["ncp"], ncrow[32:34, :], r=["ncrow"], dkey="o_ncp")
                    P.barrier()
                if stage >= 4:
                  with ExitStack() as sD:
                    sbD = lambda name, shape, dt=F32: sb(name, shape, dt, st=sD)
                    sgc = sbD("sgc", [128, 512]); sgr = sbD("sgr", [128, 512]); t1_ = sbD("t1_", [128, 512]); t2_ = sbD("t2_", [128, 512])
                    for db in range(8):
                        wt, kwt = wnext
                        for ci_, (hc, oc, n) in enumerate(TCH):
                            if ci_ == 1 and db < 7:
                                wnext = load4(tail_srcs(db + 1))
                            pp_ = [psum() for _ in range(4)]
                            srcs = (lambda c_: ycT[:, c_, oc:oc + n], lambda c_: yrT[:, c_, oc:oc + n],
                                    lambda c_: hT[:, c_, hc:hc + n], lambda c_: hT[:, c_, hc:hc + n])
                            for gi in range(4):
                                for c_ in range(8):
                                    mm(pp_[gi][0][:, 0:n], wt[:, c_, gi * 128:(gi + 1) * 128], srcs[gi](c_), c_ == 0, c_ == 7,
                                       [kwt, "ycT", "yrT"], [pp_[gi][1]])
                            (ppc, kpc), (ppr, kpr), (pgc, kgc), (pgr, kgr) = pp_
                            act(sgc[:, 0:n], pgc[:, 0:n], AF.Sigmoid, [kgc], ["sgc"])
                            act(sgr[:, 0:n], pgr[:, 0:n], AF.Sigmoid, [kgr], ["sgr"])
                            tt("vector", t1_[:, 0:n], ppc[:, 0:n], sgc[:, 0:n], ALU.mult, [kpc, "sgc"], ["t1_"])
                            tt("vector", t2_[:, 0:n], ppr[:, 0:n], sgr[:, 0:n], ALU.mult, [kpr, "sgr"], ["t2_"])
                            tt("gpsimd", mT[:, db, oc:oc + n], t1_[:, 0:n], t2_[:, 0:n], ALU.add, ["t1_", "t2_"], ["mT"])
                    P.barrier()
            if stage >= 4:
              with ExitStack() as sE:
                sbE = lambda name, shape, dt=F32: sb(name, shape, dt, st=sE)
                wo = sbE("wo", [128, 8, D], BF16)
                wos = [sbE("wos%d" % i, [128, 8, 256]) for i in range(2)]
                fgc = sbE("fgc", [128, D])
                xt2 = [sbE("xt2_%d" % i, [128, D]) for i in range(4)]
                res = [sbE("res%d" % i, [128, D]) for i in range(2)]
                yo = [sbE("yo%d" % i, [128, D]) for i in range(2)]
                junk2 = sbE("junk2", [128, D])
                ssum2 = [sbE("ssum2_%d" % i, [128, 1]) for i in range(2)]
                rstd2 = [sbE("rstd2_%d" % i, [128, 1]) for i in range(2)]
                dma(fgc[:], I["final_g"].partition_broadcast(128), w=["fgc"], dkey="fgc")
                for q in range(4):
                    sl = q % 2
                    kw = "wos%d" % sl
                    dma(wos[sl][:], I["w_o"][:, q * 256:(q + 1) * 256].rearrange("(d p) c -> p d c", p=128), w=[kw], dkey=kw)
                    cp("scalar" if q % 2 == 0 else "vector", wo[:, :, q * 256:(q + 1) * 256], wos[sl][:], [kw], ["wo"])
                def load_x(j):
                    if j < 17:
                        src = I["xs"] if j == 16 else I["xp"][j * 128:(j + 1) * 128, :]
                        dma(xt2[j % 4][0:(64 if j == 16 else 128), :], src, w=["xt2_%d" % (j % 4)], dkey="xt2_%d" % (j % 4))

                for j in range(3):
                    load_x(j)
                for i in range(17):
                    s = i % 2
                    nr = 64 if i == 16 else 128
                    oc = SEQ if i == 16 else i * 128
                    sx = i % 4
                    kx, kr_, ky, ks, krs = "xt2_%d" % sx, "res%d" % s, "yo%d" % s, "ss2_%d" % s, "rs2_%d" % s
                    load_x(i + 3)
                    pa, ka = psum()
                    pb, kb = psum()
                    for hf, (pp, kk_) in enumerate(((pa, ka), (pb, kb))):
                        for db in range(8):
                            mm(pp[0:nr, :], mT[:, db, oc:oc + nr], wo[:, db, hf * 512:(hf + 1) * 512], db == 0, db == 7, ["wo", "mT"], [kk_])
                    tt("vector", res[s][0:nr, 0:512], pa[0:nr, :], xt2[sx][0:nr, 0:512], ALU.add, [ka, kx], [kr_])
                    tt("vector", res[s][0:nr, 512:1024], pb[0:nr, :], xt2[sx][0:nr, 512:1024], ALU.add, [kb, kx], [kr_])
                    mset("gpsimd", ssum2[s][:], 0.0, [ks])
                    act(junk2[0:nr, :], res[s][0:nr, :], AF.Square, [kr_, ks], [ks], accum=ssum2[s][0:nr, :])
                    ts("vector", rstd2[s][0:nr, :], ssum2[s][0:nr, :], 1.0 / D, RMS_EPS, ALU.mult, ALU.add, [ks], [krs])
                    act(rstd2[s][0:nr, :], rstd2[s][0:nr, :], AF.Sqrt, [krs], [krs])
                    op("vector", lambda e, s=s, nr=nr: e.reciprocal(out=rstd2[s][0:nr, :], in_=rstd2[s][0:nr, :]), [krs], [krs])
                    stt("vector", yo[s][0:nr, :], res[s][0:nr, :], rstd2[s][0:nr, 0:1], fgc[0:nr, :], ALU.mult, ALU.mult, [kr_, krs, "fgc"], [ky])
                    dst = O["ys"] if i == 16 else O["yp"][i * 128:(i + 1) * 128, :]
                    dma(dst, yo[s][0:nr, :], r=[ky], dkey=ky)

        P.finish()
        nsem = P.emit()
    return nc, nsem


def shard_inputs(inp, core):
    f = lambda a: np.ascontiguousarray(np.asarray(a, dtype=np.float32))
    b0 = core * NSB
    m = {
        "xp": f(inp["x_prompt"][core]),
        "xs": f(inp["x_sample"][b0:b0 + NSB]).reshape(64, D),
        "sconv": f(inp["state_conv"][0, b0:b0 + NSB]).reshape(32, D),
        "sshift": f(inp["state_shift"][0, b0:b0 + NSB]),
        "srwkv": f(inp["state_rwkv"][0, b0:b0 + NSB]),
        "norm_g": f(inp["norm_g"][0]), "w_in": f(inp["w_in"][0]), "conv_w": f(inp["conv_w"][0]),
        "mu": f(inp["mu_shift"][0]), "w0": f(inp["w0"][0]), "w_up": f(inp["w_up"][0]), "a0": f(inp["a0"][0]),
        "a_up": f(inp["a_up"][0]), "k_k": f(inp["k_k"][0]), "k_a": f(inp["k_a"][0]),
        "r_k": f(inp["r_k"][0]).reshape(D), "ln_w": f(inp["ln_w"][0]), "ln_b": f(inp["ln_b"][0]),
        "w_out_c": f(inp["w_out_c"][0]), "w_out_r": f(inp["w_out_r"][0]), "w_o": f(inp["w_o"][0]),
        "final_g": f(inp["final_g"]),
    }
    return m


_CACHE = {}


def kernel(**inputs):
    if "nc" not in _CACHE:
        _CACHE["nc"] = build_program()[0]
    nc = _CACHE["nc"]
    in_maps = [shard_inputs(inputs, c) for c in range(8)]
    res = run_bass_kernel_spmd(nc, in_maps, core_ids=list(range(8)))
    R = res.results
    yp = np.stack([R[c]["yp"] for c in range(8)])
    ys = np.concatenate([R[c]["ys"].reshape(NSB, 4, D) for c in range(8)])
    ncp = np.stack([R[c]["ncp"] for c in range(8)])[None]
    nsp = np.stack([R[c]["nsp"].reshape(D) for c in range(8)])[None]
    nrp = np.stack([R[c]["nrp"] for c in range(8)])[None]
    ncs = np.concatenate([R[c]["ncs"].reshape(NSB, 2, D) for c in range(8)])[None]
    nss = np.concatenate([R[c]["nss"] for c in range(8)])[None]
    nrs = np.concatenate([R[c]["nrs"] for c in range(8)])[None]
    return tuple(np.ascontiguousarray(a, dtype=np.float32) for a in (yp, ys, ncp, nsp, nrp, ncs, nss, nrs))
```

```python
import numpy as np
from contextlib import ExitStack
import concourse.bass as bass
import concourse.mybir as mybir
from concourse.bass_utils import run_bass_kernel_spmd

F32 = mybir.dt.float32
BF16 = mybir.dt.bfloat16
AF = mybir.ActivationFunctionType
ALU = mybir.AluOpType

ENGS = ("tensor", "vector", "scalar", "gpsimd", "sync")
SEM_EPOCH = 30000


class _Op:
    __slots__ = ("eng", "fn", "reads", "writes", "dma", "deps", "pos", "marked", "ev", "dkey", "waits", "barriered", "weak")

    def __init__(self, eng, fn, reads, writes, dma, dkey):
        self.eng, self.fn, self.reads, self.writes, self.dma, self.dkey = eng, fn, reads, writes, dma, dkey
        self.deps = []
        self.marked = False
        self.ev = None
        self.waits = []
        self.barriered = False


class _DB:
    def __init__(self, tiles, prog, attr="par"):
        self.tiles, self.prog, self.attr = tiles, prog, attr

    def __getitem__(self, idx):
        return self.tiles[getattr(self.prog, self.attr)][idx]


class Prog:
    def __init__(self, nc):
        self.nc = nc
        self.ops = []
        self.last_w = {}
        self.readers = {}
        self.barrier_deps = {e: [] for e in ENGS}
        self.per_eng = {e: [] for e in ENGS}
        self.dma_ops = []
        self.par = 0
        self.dbkeys = set()
        self.cpar = 0
        self.cdbkeys = set()

    def op(self, eng, fn, reads=(), writes=(), dma=False, dkey=None):
        km = lambda k: ("%s@%d" % (k, self.par)) if k in self.dbkeys else (("%s#%d" % (k, self.cpar)) if k in self.cdbkeys else k)
        o = _Op(eng, fn, tuple(km(k) for k in reads), tuple(km(k) for k in writes), dma, dkey)
        deps = set()
        raw = set()
        for k in o.reads:
            w = self.last_w.get(k)
            if w is not None:
                deps.add(w)
                raw.add(w)
        for k in o.writes:
            w = self.last_w.get(k)
            if w is not None:
                deps.add(w)
            for r in self.readers.get(k, ()):
                deps.add(r)
        for d in self.barrier_deps[eng]:
            deps.add(d)
            raw.add(d)
        self.barrier_deps[eng] = []
        o.deps = list(deps)
        o.weak = deps - raw
        for k in o.writes:
            self.last_w[k] = o
            self.readers[k] = []
        for k in o.reads:
            if k not in o.writes:
                self.readers.setdefault(k, []).append(o)
        o.pos = len(self.per_eng[eng])
        self.per_eng[eng].append(o)
        self.ops.append(o)
        if dma:
            self.dma_ops.append(o)
        return o

    def barrier(self):
        lasts = []
        for e in ENGS:
            for o in reversed(self.per_eng[e]):
                if not o.dma and o.fn is not None:
                    lasts.append(o)
                    break
        lasts += [o for o in self.dma_ops if not o.barriered]
        for o in self.dma_ops:
            o.barriered = True
        for e in ENGS:
            self.barrier_deps[e] = list(lasts)
        self.last_w = {}
        self.readers = {}

    def finish(self, final_eng="sync"):
        o = self.op(final_eng, None)
        o.deps = list(self.dma_ops)

    def emit(self):
        nc = self.nc
        waited_pos = {e: {x: -1 for x in ENGS} for e in ENGS}
        dma_cnt = {}
        waited_dma = {e: {} for e in ENGS}
        for o in self.ops:
            if o.dma:
                dma_cnt[o.dkey] = dma_cnt.get(o.dkey, 0) + 16
                o.ev = ("d_%s" % o.dkey, dma_cnt[o.dkey])
                o.marked = True
        for o in self.ops:
            e = o.eng
            need = []
            for d in o.deps:
                if d.dma:
                    nm, val = d.ev
                    if waited_dma[e].get(nm, 0) < val:
                        waited_dma[e][nm] = val
                        need.append(d)
                else:
                    if d.fn is None:
                        continue
                    if d.eng == e and e == "tensor":
                        continue
                    if d.eng == e and e in ("vector", "scalar", "gpsimd") and d in o.weak and not o.dma:
                        continue
                    if waited_pos[e][d.eng] < d.pos:
                        waited_pos[e][d.eng] = d.pos
                        need.append(d)
            o.waits = need
            for d in need:
                d.marked = True
        cnt = {e: 0 for e in ENGS}
        for o in self.ops:
            if o.dma or not o.marked:
                continue
            cnt[o.eng] += 1
            ep, tk = divmod(cnt[o.eng] - 1, SEM_EPOCH)
            o.ev = ("s_%s_%d" % (o.eng, ep), tk + 1)
        sems = {}
        with ExitStack() as st:
            for o in self.ops:
                if o.marked and o.ev[0] not in sems:
                    sems[o.ev[0]] = st.enter_context(nc.semaphore(o.ev[0]))
            with nc.Block() as block:
                def run(engname):
                    def body(eng):
                        for o in self.per_eng[engname]:
                            best = {}
                            for d in o.waits:
                                nm, val = d.ev
                                if best.get(nm, 0) < val:
                                    best[nm] = val
                            for nm, val in best.items():
                                eng.wait_ge(sems[nm], val)
                            if o.fn is None:
                                continue
                            ins = o.fn(eng)
                            if o.marked:
                                ins.then_inc(sems[o.ev[0]], 16 if o.dma else 1)
                    return body
                block.tensor(run("tensor"))
                block.vector(run("vector"))
                block.scalar(run("scalar"))
                block.gpsimd(run("gpsimd"))
                block.sync(run("sync"))
        return len(sems)


D = 1024
NIN = 10368
SEQ = 2048
NSB = 16
C = 64
G = 128
SC0 = 1 + SEQ
HCOLS = SC0 + 80
NTOK = SEQ + 64
LD = 0.6065306597126334
RMS_EPS = 1e-6
GN_EPS = 64e-5
CB_XIN, CB_B, CB_C, CB_ZC = 0, 1024, 2048, 3072
CB_R = 4096
CB_GC, CB_GR = 8320, 9344

IN_SPECS = [
    ("xp", [SEQ, D]), ("xs", [64, D]), ("sconv", [32, D]), ("sshift", [16, D]),
    ("srwkv", [NSB, 16, 64, 64]), ("norm_g", [D]), ("w_in", [D, NIN]), ("conv_w", [3, D]),
    ("mu", [4224]), ("w0", [D]), ("w_up", [64, D]), ("a0", [D]), ("a_up", [64, D]),
    ("k_k", [D]), ("k_a", [D]), ("r_k", [D]), ("ln_w", [D]), ("ln_b", [D]),
    ("w_out_c", [D, D]), ("w_out_r", [D, D]), ("w_o", [D, D]), ("final_g", [D]),
]
OUT_SPECS = [
    ("yp", [SEQ, D]), ("ys", [64, D]), ("ncp", [2, D]), ("nsp", [1, D]), ("nrp", [16, 64, 64]),
    ("ncs", [32, D]), ("nss", [16, D]), ("nrs", [NSB, 16, 64, 64]),
]


def build_program(stage=99, dbg=False):
    nc = bass.Bass("TRN2", target_bir_lowering=False)
    I = {n: nc.dram_tensor(n, s, F32, kind="ExternalInput").ap() for n, s in IN_SPECS}
    O = {n: nc.dram_tensor(n, s, F32, kind="ExternalOutput").ap() for n, s in OUT_SPECS}
    DBG = {}
    P = Prog(nc)

    def op(eng, fn, r=(), w=()):
        return P.op(eng, fn, reads=r, writes=w)

    def dma(out, in_, r=(), w=(), dkey=None, eng="sync"):
        return P.op("sync", lambda e: e.dma_start(out=out, in_=in_), reads=r, writes=w, dma=True, dkey=dkey)

    def mm(out, lhsT, rhs, start, stop, r, w):
        return P.op("tensor", lambda e: e.matmul(out, lhsT=lhsT, rhs=rhs, start=start, stop=stop), reads=r, writes=w)

    def tr(out, in_, ident, r, w):
        return P.op("tensor", lambda e: e.transpose(out, in_, ident), reads=r, writes=w)

    def act(out, in_, func, r, w, bias=None, scale=None, accum=None):
        kw = {}
        if bias is not None:
            kw["bias"] = bias
        if scale is not None:
            kw["scale"] = scale
        if accum is not None:
            kw["accum_out"] = accum
        return P.op("scalar", lambda e: e.activation(out=out, in_=in_, func=func, **kw), reads=r, writes=w)

    def tt(eng, out, in0, in1, o, r, w):
        return P.op(eng, lambda e: e.tensor_tensor(out=out, in0=in0, in1=in1, op=o), reads=r, writes=w)

    def ts(eng, out, in0, s1, s2, o0, o1, r, w):
        return P.op(eng, lambda e: e.tensor_scalar(out=out, in0=in0, scalar1=s1, scalar2=s2, op0=o0, op1=o1), reads=r, writes=w)

    def ts1(eng, out, in_, sc, o, r, w):
        return P.op(eng, lambda e: e.tensor_single_scalar(out=out, in_=in_, scalar=sc, op=o), reads=r, writes=w)

    def stt(eng, out, in0, sc, in1, o0, o1, r, w):
        return P.op(eng, lambda e: e.scalar_tensor_tensor(out=out, in0=in0, scalar=sc, in1=in1, op0=o0, op1=o1), reads=r, writes=w)

    def cp(eng, out, in_, r, w):
        if eng == "scalar":
            return P.op(eng, lambda e: e.copy(out=out, in_=in_), reads=r, writes=w)
        return P.op(eng, lambda e: e.tensor_copy(out=out, in_=in_), reads=r, writes=w)

    def mset(eng, ap, val, w):
        return P.op(eng, lambda e: e.memset(ap, val), writes=w)

    with ExitStack() as top:
        def sb(name, shape, dt=F32, st=top):
            return st.enter_context(nc.sbuf_tensor(name, shape, dt))

        psb = [top.enter_context(nc.psum_tensor("psb%d" % i, [128, 512], F32)) for i in range(8)]
        pstate = {"i": 0}

        def psum():
            i = pstate["i"]
            pstate["i"] = (i + 1) % 8
            return psb[i], "ps%d" % i

        ident = sb("ident", [128, 128])
        bones = sb("bones", [128, 128])
        hT = sb("hT", [128, 8, HCOLS], BF16)
        prm = sb("prm", [128, 128])
        ucb = sb("ucb", [128, 8, 32])

        mset("gpsimd", ident[:], 1.0, ["ident"])
        op("gpsimd", lambda e: e.affine_select(out=ident[:], in_=ident[:], pattern=[[-1, 128]], compare_op=ALU.is_equal,
                                               fill=0.0, base=0, channel_multiplier=1), ["ident"], ["ident"])
        mset("gpsimd", bones[:], 0.0, ["bones"])
        mset("gpsimd", bones[0:64, 0:64], 1.0, ["bones"])
        mset("gpsimd", bones[64:128, 64:128], 1.0, ["bones"])
        mset("gpsimd", hT[:, :, 0:1], 0.0, ["hT0"])

        PR_MU, PR_W0, PR_A0, PR_KK, PR_KA, PR_RK, PR_LNW, PR_LNB, PR_CW = 0, 33, 41, 49, 57, 65, 73, 81, 89
        with ExitStack() as s0:
            prow = sb("prow", [128, 128], st=s0)
            mset("gpsimd", prow[:], 0.0, ["prow"])
            dma(prow[0:33, :], I["mu"].rearrange("(r c) -> r c", c=128), r=["prow"], w=["prow"], dkey="prow")
            for nm, base in (("w0", PR_W0), ("a0", PR_A0), ("k_k", PR_KK), ("k_a", PR_KA), ("r_k", PR_RK),
                             ("ln_w", PR_LNW), ("ln_b", PR_LNB)):
                dma(prow[base:base + 8, :], I[nm].rearrange("(r c) -> r c", c=128), r=["prow"], w=["prow"], dkey="prow")
            dma(prow[PR_CW:PR_CW + 24, :], I["conv_w"].rearrange("j (r c) -> (j r) c", c=128), r=["prow"], w=["prow"], dkey="prow")
            pb, pk = psum()
            tr(pb[:, 0:128], prow[:], ident[:], ["prow", "ident"], [pk])
            cp("vector", prm[:], pb[:, 0:128], [pk], ["prm"])
            P.barrier()

        sAr = ExitStack()
        arena = sAr.enter_context(nc.sbuf_tensor("arena", [128, 13056], F32))
        wr = arena[:, :].bitcast(BF16)[:, 0:17408].rearrange("p (d n) -> p d n", n=2176)
        wst = [arena[:, 8704 + i * 2176:8704 + (i + 1) * 2176] for i in range(2)]

        def pass_nbl(hp):
            return [(CB_R + (4 * hp + b) * 128, 4 * hp + b) for b in range(4)] \
                + [(CB_R + 1024 + (4 * hp + b) * 128, 8 + 4 * hp + b) for b in range(4)] \
                + [(CB_R + 2048 + (4 * hp + b) * 128, 16 + 4 * hp + b) for b in range(4)] \
                + [(CB_R + 3072, 24)] \
                + [(CB_R + 3200 + (4 * hp + b) * 128, 25 + 4 * hp + b) for b in range(4)]

        def load_wr_dma(hp, db):
            nbl = pass_nbl(hp)
            s = db % 2
            kw = "wst%d" % s
            rows = slice(db * 128, (db + 1) * 128)
            for (lo, n, col) in ((0, 512, nbl[0][0]), (512, 512, nbl[4][0]), (1024, 512, nbl[8][0]),
                                 (1536, 128, nbl[12][0]), (1664, 512, nbl[13][0])):
                dma(wst[s][:, lo:lo + n], I["w_in"][rows, col:col + n], w=[kw], dkey=kw)

        def load_wr_cast(hp, db):
            s = db % 2
            kw = "wst%d" % s
            cp("scalar", wr[:, db, 0:1088], wst[s][:, 0:1088], [kw], ["wr"])
            cp("vector", wr[:, db, 1088:2176], wst[s][:, 1088:2176], [kw], ["wr"])

        def load_wr(hp, db):
            load_wr_dma(hp, db)
            load_wr_cast(hp, db)

        with ExitStack() as s1:
            gbc = sb("gbc", [128, D], st=s1)
            dma(gbc[:], I["norm_g"].partition_broadcast(128), w=["gbc"], dkey="gbc")
            xt = [sb("xt%d" % i, [128, D], st=s1) for i in range(4)]
            ht = [sb("ht%d" % i, [128, D], st=s1) for i in range(2)]
            junk = sb("junk", [128, D], st=s1)
            ssum = [sb("ssum%d" % i, [128, 1], st=s1) for i in range(2)]
            rstd = [sb("rstd%d" % i, [128, 1], st=s1) for i in range(2)]
            def stage_a(i):
                if i % 2 == 0 and i < 16:
                    load_wr_dma(0, i // 2)
                if i % 2 == 0 and 2 <= i:
                    load_wr_cast(0, i // 2 - 1)
                s = i % 2
                sx = i % 4
                kx, kh, ks, kr = "xt%d" % sx, "ht%d" % s, "ss%d" % s, "rs%d" % s
                if i < 16:
                    dma(xt[sx][:], I["xp"][i * 128:(i + 1) * 128, :], w=[kx], dkey=kx)
                else:
                    mset("gpsimd", xt[sx][:], 0.0, [kx])
                    dma(xt[sx][0:64, :], I["xs"], r=[kx], w=[kx], dkey=kx)
                    dma(xt[sx][64:80, :], I["sshift"], r=[kx], w=[kx], dkey=kx)
                    dma(xt[sx][80:112, :], I["sconv"], r=[kx], w=[kx], dkey=kx)
                mset("gpsimd", ssum[s][:], 0.0, [ks])
                act(junk[:], xt[sx][:], AF.Square, [kx, ks], [ks], accum=ssum[s][:])
                ts("vector", rstd[s][:], ssum[s][:], 1.0 / D, RMS_EPS, ALU.mult, ALU.add, [ks], [kr])
                act(rstd[s][:], rstd[s][:], AF.Sqrt, [kr], [kr])
                op("vector", lambda e, s=s: e.reciprocal(out=rstd[s][:], in_=rstd[s][:]), [kr], [kr])
                if i < 16:
                    stt("vector", ht[s][:], xt[sx][:], rstd[s][:, 0:1], gbc[:], ALU.mult, ALU.mult, [kx, kr, "gbc"], [kh])
                else:
                    stt("vector", ht[s][0:64, :], xt[sx][0:64, :], rstd[s][0:64, 0:1], gbc[0:64, :], ALU.mult, ALU.mult,
                        [kx, kr, "gbc"], [kh])
                    cp("gpsimd", ht[s][64:128, :], xt[sx][64:128, :], [kx, kh], [kh])

            def stage_b(i):
                s = i % 2
                sx = i % 4
                kx, kh, ks, kr = "xt%d" % sx, "ht%d" % s, "ss%d" % s, "rs%d" % s
                pa, ka = psum()
                pb, kb = psum()
                for j in range(8):
                    pp, kk_ = (pa, ka) if j < 4 else (pb, kb)
                    tr(pp[:, (j % 4) * 128:(j % 4 + 1) * 128], ht[s][:, j * 128:(j + 1) * 128], ident[:], [kh, "ident"], [kk_])
                pa3 = pa[:].rearrange("p (j c) -> p j c", c=128)
                pb3 = pb[:].rearrange("p (j c) -> p j c", c=128)
                if i < 16:
                    c0 = 1 + i * 128
                    cp("scalar", hT[:, 0:4, c0:c0 + 128], pa3, [ka], ["hT%d" % i])
                    cp("vector", hT[:, 4:8, c0:c0 + 128], pb3, [kb], ["hT%d" % i])
                    if i == 15:
                        dma(O["nsp"], ht[s][127:128, :], r=[kh], dkey="o_nsp")
                else:
                    cp("scalar", hT[:, 0:4, SC0:SC0 + 80], pa3[:, :, 0:80], [ka], ["hT16"])
                    cp("vector", hT[:, 4:8, SC0:SC0 + 80], pb3[:, :, 0:80], [kb], ["hT16"])
                    cp("scalar", ucb[:, 0:4, :], pa3[:, :, 80:112], [ka], ["ucb"])
                    cp("vector", ucb[:, 4:8, :], pb3[:, :, 80:112], [kb], ["ucb"])
                    dma(O["nss"], ht[s][3:64:4, :], r=[kh], dkey="o_nss")

            stage_a(0)
            for i in range(17):
                if i + 1 < 17:
                    stage_a(i + 1)
                stage_b(i)
            P.barrier()
        if dbg:
            DBG["hT"] = nc.dram_tensor("dbg_hT", [128, 8, HCOLS], BF16, kind="ExternalOutput").ap()
            dma(DBG["hT"], hT[:], dkey="dbg_hT")
            DBG["prm"] = nc.dram_tensor("dbg_prm", [128, 128], F32, kind="ExternalOutput").ap()
            dma(DBG["prm"], prm[:], dkey="dbg_prm")
            DBG["ucb"] = nc.dram_tensor("dbg_ucb", [128, 8, 32], F32, kind="ExternalOutput").ap()
            dma(DBG["ucb"], ucb[:], dkey="dbg_ucb")


        yr_d = nc.dram_tensor("yr_d", [128, 8, NTOK], BF16).ap()
        if stage >= 2:
          with ExitStack() as s2:
            sb2 = lambda name, shape, dt=F32: sb(name, shape, dt, st=s2)
            loraW = sb2("loraW", [128, 512]); loraA = sb2("loraA", [128, 512])
            mset("gpsimd", loraW[64:128, :], 0.0, ["lora"])
            mset("gpsimd", loraA[0:64, :], 0.0, ["lora"])
            pm = sb2("pm", [128, 2])
            mset("gpsimd", pm[:], 0.0, ["pm"])
            mset("gpsimd", pm[0:64, 0:1], 1.0, ["pm"])
            mset("gpsimd", pm[64:128, 1:2], 1.0, ["pm"])
            omu = sb2("omu", [128, 33])
            ts("vector", omu[:], prm[:, PR_MU:PR_MU + 33], -1.0, 1.0, ALU.mult, ALU.add, ["prm"], ["omu"])
            def dup(name, base, f=None):
                t = sb2(name, [128, 16])
                src = prm[:, base:base + 8].unsqueeze(2).to_broadcast([128, 8, 2])
                dst = t[:].rearrange("p (b c) -> p b c", c=2)
                if f is None:
                    cp("vector", dst, src, ["prm"], [name])
                else:
                    ts("vector", dst, src, f[0], f[1], ALU.mult, ALU.add, ["prm"], [name])
                return t
            kk2 = dup("kk2", PR_KK); ka2 = dup("ka2", PR_KA); omka2 = dup("omka2", PR_KA, (-1.0, 1.0))
            rk2 = dup("rk2", PR_RK); lnw2 = dup("lnw2", PR_LNW); lnb2 = dup("lnb2", PR_LNB)
            epsg = sb2("epsg", [128, 1]); mset("gpsimd", epsg[:], GN_EPS, ["epsg"])
            selT = sb2("selT", [8, 4, 128])
            mset("gpsimd", selT[:], 1.0, ["selT"])
            op("gpsimd", lambda e: e.affine_select(out=selT[:, :, 0:64], in_=selT[:, :, 0:64], pattern=[[-2, 4], [0, 64]],
                                                   compare_op=ALU.is_equal, fill=0.0, base=0, channel_multiplier=1), ["selT"], ["selT"])
            op("gpsimd", lambda e: e.affine_select(out=selT[:, :, 64:128], in_=selT[:, :, 64:128], pattern=[[-2, 4], [0, 64]],
                                                   compare_op=ALU.is_equal, fill=0.0, base=-1, channel_multiplier=1), ["selT"], ["selT"])
            rnt = [sb2("rnt%d" % i, [128, 8]) for i in range(2)]
            rnT = [sb2("rnT%d" % i, [8, 128]) for i in range(2)]
            bd4 = sb2("bd4", [128, 64])
            mset("gpsimd", bd4[:], 1.0, ["bd4"])
            def asel(ap, pattern, cmpop, base, cm, key):
                op("gpsimd", lambda e: e.affine_select(out=ap, in_=ap, pattern=pattern, compare_op=cmpop, fill=0.0,
                                                       base=base, channel_multiplier=cm), [key], [key])
            for half in range(2):
                v = bd4[half * 64:(half + 1) * 64, :].rearrange("p (b t) -> p b t", t=4)
                asel(v, [[-4, 16], [0, 4]], ALU.is_ge, 0, 1, "bd4")
                asel(v, [[4, 16], [0, 4]], ALU.is_ge, 3, -1, "bd4")
            maskT = [sb2("maskT%d" % i, [128, 128]) for i in range(2)]
            maskN = [sb2("maskN%d" % i, [64, 64]) for i in range(2)]
            Tcat = [sb2("Tcat%d" % i, [64, 128]) for i in range(2)]
            Tsuf = [sb2("Tsuf%d" % i, [64, 128]) for i in range(2)]
            mset("gpsimd", maskT[0][:], 1.0, ["mT"])
            for half in range(2):
                asel(maskT[0][half * 64:(half + 1) * 64, 0:64], [[1, 64]], ALU.is_gt, 0, -1, "mT")
                asel(maskT[0][half * 64:(half + 1) * 64, 64:128], [[1, 64]], ALU.is_ge, 0, -1, "mT")
            mset("gpsimd", maskN[0][:], 1.0, ["mN"])
            asel(maskN[0][:], [[-1, 64]], ALU.is_gt, 0, 1, "mN")
            mset("gpsimd", Tcat[0][:], -LD, ["Tc"])
            asel(Tcat[0][:, 0:64], [[1, 64]], ALU.is_ge, 0, -1, "Tc")
            asel(Tcat[0][:, 64:128], [[1, 64]], ALU.is_gt, 0, -1, "Tc")
            mset("gpsimd", Tsuf[0][:], -LD, ["Tsf"])
            asel(Tsuf[0][:, 0:64], [[-1, 64]], ALU.is_gt, 0, 1, "Tsf")
            asel(Tsuf[0][:, 64:128], [[-1, 64]], ALU.is_gt, 0, 1, "Tsf")
            for lo in (0, 64):
                tt("gpsimd", maskT[1][:, lo:lo + 64], maskT[0][:, lo:lo + 64], bd4[:], ALU.mult, ["mT", "bd4"], ["mTs"])
                tt("gpsimd", Tcat[1][:, lo:lo + 64], Tcat[0][:, lo:lo + 64], bd4[0:64, :], ALU.mult, ["Tc", "bd4"], ["Tcs"])
                tt("gpsimd", Tsuf[1][:, lo:lo + 64], Tsuf[0][:, lo:lo + 64], bd4[0:64, :], ALU.mult, ["Tsf", "bd4"], ["Tsfs"])
            tt("gpsimd", maskN[1][:], maskN[0][:], bd4[0:64, :], ALU.mult, ["mN", "bd4"], ["mNs"])
            cmask = sb2("cmask", [128, 16, 64])
            mset("gpsimd", cmask[:], 1.0, ["cmask"])
            asel(cmask[:], [[-4, 16], [1, 64]], ALU.is_ge, 0, 0, "cmask")
            asel(cmask[:], [[4, 16], [-1, 64]], ALU.is_ge, 3, 0, "cmask")
            idP = [sb2("idP%d" % i, [128, 64]) for i in range(2)]
            mset("gpsimd", idP[0][:], 0.0, ["idP"]); mset("gpsimd", idP[1][:], 0.0, ["idP"])
            cp("gpsimd", idP[0][0:64, :], ident[0:64, 0:64], ["ident", "idP"], ["idP"])
            cp("gpsimd", idP[1][64:128, :], ident[64:128, 64:128], ["ident", "idP"], ["idP"])
            P.barrier()

            T4 = lambda name: sb2(name, [128, 4, 128])
            CD = lambda name, shape, dt=F32: _DB([sb2("%s_c%d" % (name, i), shape, dt) for i in range(2)], P, "cpar")
            kT, aT, kkT, x1, gcen, gx1 = [T4(n) for n in ("kT", "aT", "kkT", "x1", "gcen", "gx1")]
            x2 = x1
            ysT = _DB([T4("ysT_%d" % i) for i in range(2)], P)
            DBN = ("rT", "zT", "nkk", "bonT")
            rT, zT, nkk, bonT = [_DB([T4("%s_%d" % (n, i)) for i in range(2)], P) for n in DBN]
            svS = _DB([sb2("svS_%d" % i, [128, 8, 128]) for i in range(2)], P)
            bkS = _DB([sb2("bkS_%d" % i, [128, 8, 128]) for i in range(2)], P)
            P.dbkeys = set(DBN) | {"svS", "bkS", "ysTh0", "ysTh1", "yrg"}
            P.cdbkeys = {"sig_tm", "UVv", "UVuh0", "UVuh1", "eI", "eNI", "eE", "esuf", "BKh", "ARa", "ARr", "BKt", "ATm"} | \
                {"%s%d%s" % (a_, i_, h_) for a_ in ("X", "XT", "TT") for i_ in range(2) for h_ in ("h0", "h1")}
            cen = gcen
            wa = sb2("wa", [128, 128])
            tmp1 = [sb2("tmp1_%d" % i, [128, 128]) for i in range(2)]
            sig_tm = CD("sig_tm", [64, 512])
            UV = CD("UV", [128, 8, 64], BF16); ATm = CD("ATm", [128, 8, 128], BF16)
            AR = CD("AR", [128, 4, 128], BF16)
            BKtP = [CD("BKtP%d" % i, [128, 4, 128], BF16) for i in range(2)]
            VP = CD("VP", [128, 8, 64], BF16)
            for c_ in range(2):
                mset("gpsimd", BKtP[0].tiles[c_][64:128], 0.0, ["cinit"]); mset("gpsimd", BKtP[1].tiles[c_][0:64], 0.0, ["cinit"])
                mset("gpsimd", VP.tiles[c_][0:64], 0.0, ["cinit"])
            yrg = _DB([sb2("yrg%d" % i, [128, 4, 128], BF16) for i in range(2)], P)
            HPd = [sb2("HPd%d" % i, [128, 4, 64], BF16) for i in range(2)]
            eI = CD("eI", [128, 4, 64]); eNI = CD("eNI", [128, 4, 64]); eE = CD("eE", [128, 4, 64])
            esuf = CD("esuf", [128, 512]); BKh = CD("BKh", [128, 512], BF16)
            Xb = [sb2("Xb%d" % i, [64, 8, 2, 64], BF16) for i in range(2)]
            XTb = [sb2("XTb%d" % i, [64, 8, 2, 64], BF16) for i in range(2)]
            TTb = [sb2("TTb%d" % i, [64, 8, 2, 64], BF16) for i in range(2)]
            Wsb = sb2("Wsb", [64, 8, 64], BF16)
            H = sb2("H", [128, 4, 64]); HP = sb2("HP", [128, 4, 64])
            H0T = arena[:, 0:4096].rearrange("p (b i v) -> p b i v", i=4, v=64)
            arena_b = arena[:, :].bitcast(BF16)
            ARm = [arena_b[:, 8192 + j * 1024:8192 + (j + 1) * 1024].rearrange("p (b s) -> p b s", s=64) for j in range(4)]
            H0Tb = arena_b[:, 12288:16384].rearrange("p (b i v) -> p b i v", i=4, v=64)
            h3 = lambda ap: ap.rearrange("p (h x) -> p h x", x=64)
            S0v = [h3(arena[0:64, 8192 + i * 512:8192 + (i + 1) * 512]) for i in range(2)]
            UVm = [h3(arena_b[:, 18432 + i * 1024:18432 + i * 1024 + 512]) for i in range(2)]
            Sout = [h3(arena[0:64, 10240 + i * 512:10240 + (i + 1) * 512]) for i in range(2)]
            cnt = {"t1": 0, "ev": 0}
            P.barrier()

            def v3(ap):
                return ap.rearrange("p (c s) -> p c s", s=64)

            def evac(out, in_, r, w):
                cnt["ev"] += 1
                cp("scalar" if cnt["ev"] % 2 else "vector", out, in_, r, w)

            for hp in range(2):
                nbl = pass_nbl(hp)
                if hp > 0:
                    for db in range(8):
                        load_wr(hp, db)
                dma(loraW[0:64, :], I["w_up"][:, 512 * hp:512 * hp + 512], r=["lora"], w=["lora"], dkey="lora")
                dma(loraA[64:128, :], I["a_up"][:, 512 * hp:512 * hp + 512], r=["lora"], w=["lora"], dkey="lora")
                mset("gpsimd", H[:], 0.0, ["Hh0", "Hh1"])
                mset("gpsimd", HPd[0][:], 0.0, ["HPdh0", "HPdh1"]); mset("gpsimd", HPd[1][:], 0.0, ["HPdh0", "HPdh1"])

                def front(g):
                    smp = g == 16
                    nt = 64 if smp else 128
                    nch = 1 if smp else 2
                    m = 1 if smp else 0
                    Q = 4 if smp else 8
                    p8 = slice(8 * hp, 8 * hp + 8, 2) if smp else slice(8 * hp, 8 * hp + 8)

                    def gv(t, smp=smp):
                        return t[:, :, 0:64] if smp else t[:, :, :].rearrange("p b (c s) -> p (b c) s", s=64)

                    def sv(S, lo, smp=smp):
                        return S[:, 0:8:2, lo:lo + 64] if smp else S[:, :, lo:lo + 64]

                    def pbc(t, p8=p8, Q=Q):
                        return t[:, p8].unsqueeze(2).to_broadcast([128, Q, 64])

                    for li, (col, mrow) in enumerate(nbl):
                        pb, pk = psum()
                        if smp:
                            rhs_cols = slice(SC0, SC0 + 80)
                            ncol = 80
                        else:
                            rhs_cols = slice(g * 128, g * 128 + 129)
                            ncol = 129
                        for db in range(8):
                            mm(pb[:, 0:ncol], wr[:, db, li * 128:(li + 1) * 128], hT[:, db, rhs_cols], db == 0, db == 7, ["wr"], [pk])
                        t1 = tmp1[cnt["t1"] % 2]; k1 = "tmp1_%d" % (cnt["t1"] % 2); cnt["t1"] += 1
                        cur = pb[:, 0:64] if smp else pb[:, 1:129]
                        op("scalar", lambda e, t1=t1, cur=cur, mrow=mrow, nt=nt: e.mul(out=t1[:, 0:nt], in_=cur, mul=omu[:, mrow:mrow + 1]),
                           [pk, "omu"], [k1])
                        if li < 4:
                            dst, dk = rT[:, li, 0:nt], "rT"
                        elif li < 8:
                            dst, dk = kT[:, li - 4, 0:nt], "kT"
                        elif li < 12:
                            dst, dk = None, "svS"
                        elif li == 12:
                            dst, dk = wa[:, 0:nt], "wa"
                        else:
                            dst, dk = zT[:, li - 13, 0:nt], "zT"
                        musc = prm[:, PR_MU + mrow:PR_MU + mrow + 1]
                        if not smp:
                            if dst is None:
                                bi = li - 8
                                stt("vector", svS[:, 2 * bi:2 * bi + 2, 64:128], v3(pb[:, 0:128]), musc, v3(t1[:, 0:128]),
                                    ALU.mult, ALU.add, [pk, k1, "prm"], [dk])
                            else:
                                stt("vector", dst, pb[:, 0:128], musc, t1[:, 0:128], ALU.mult, ALU.add, [pk, k1, "prm"], [dk])
                        else:
                            if dst is None:
                                dst = svS[:, 2 * (li - 8), 64:128]
                            d3 = dst.rearrange("p (b t) -> p b t", t=4)
                            c3 = pb[:, 0:64].rearrange("p (b t) -> p b t", t=4)
                            t3 = t1[:, 0:64].rearrange("p (b t) -> p b t", t=4)
                            stt("vector", d3[:, :, 1:4], c3[:, :, 0:3], musc, t3[:, :, 1:4], ALU.mult, ALU.add, [pk, k1, "prm"], [dk])
                            stt("vector", d3[:, :, 0:1], pb[:, 64:80].unsqueeze(2), musc, t3[:, :, 0:1], ALU.mult, ALU.add,
                                [pk, k1, "prm"], [dk])
                        yield

                    act(wa[0:64, 0:nt], wa[0:64, 0:nt], AF.Tanh, ["wa"], ["wa"])
                    yield
                    for bi in range(4):
                        blk = 4 * hp + bi
                        pb, pk = psum()
                        mm(pb[:, 0:nt], loraW[:, bi * 128:(bi + 1) * 128], wa[:, 0:nt], True, True, ["lora", "wa"], [pk])
                        mm(pb[:, 128:128 + nt], loraA[:, bi * 128:(bi + 1) * 128], wa[:, 0:nt], True, True, ["lora", "wa"], [pk])
                        so = svS[:, 2 * bi, 0:64] if smp else svS[:, 2 * bi:2 * bi + 2, 0:64]
                        si = pb[:, 0:64] if smp else v3(pb[:, 0:128])
                        act(so, si, AF.Sigmoid, [pk, "prm"], ["svS"], bias=prm[:, PR_W0 + blk:PR_W0 + blk + 1], scale=1.0)
                        act(aT[:, bi, 0:nt], pb[:, 128:128 + nt], AF.Sigmoid, [pk, "prm"], ["aT"],
                            bias=prm[:, PR_A0 + blk:PR_A0 + blk + 1], scale=1.0)
                        yield
                    tt("gpsimd", gv(kkT), gv(kT), pbc(kk2), ALU.mult, ["kT", "kk2"], ["kkT"])
                    tt("gpsimd", gv(x1), gv(kkT), gv(kkT), ALU.mult, ["kkT"], ["x1"])
                    yield
                    pc, kc = psum()
                    for bi in range(4):
                        mm(pc[0:nt, 2 * bi:2 * bi + 2], x1[:, bi, 0:nt], pm[:, 0:2], True, True, ["x1", "pm"], [kc])
                    act(rnt[0][0:nt, :], pc[0:nt, 0:8], AF.Sqrt, [kc], ["rnt0"])
                    yield
                    ts1("vector", rnt[0][0:nt, :], rnt[0][0:nt, :], 1e-12, ALU.max, ["rnt0"], ["rnt0"])
                    op("vector", lambda e, nt=nt: e.reciprocal(out=rnt[0][0:nt, :], in_=rnt[0][0:nt, :]), ["rnt0"], ["rnt0"])
                    pt, kt = psum()
                    tr(pt[0:8, 0:nt], rnt[0][0:nt, :], ident[0:nt, 0:nt], ["rnt0", "ident"], [kt])
                    cp("scalar", rnT[0][:, 0:nt], pt[0:8, 0:nt], [kt], ["rnT0"])
                    yield
                    pb, pk = psum()
                    for bi in range(4):
                        mm(pb[:, bi * nt:(bi + 1) * nt], selT[:, bi, :], rnT[0][:, 0:nt], True, True, ["selT", "rnT0"], [pk])
                    pbn = pb[:, 0:256].rearrange("p (b s) -> p b s", s=64) if smp else pb[:, 0:512].rearrange("p (q s) -> p q s", s=64)
                    stt("vector", gv(nkk), gv(kkT), -1.0, pbn, ALU.mult, ALU.mult, ["kkT", pk], ["nkk"])
                    yield
                    tt("gpsimd", gv(x2), gv(aT), pbc(ka2), ALU.mult, ["aT", "ka2"], ["x1"])
                    tt("gpsimd", gv(x2), gv(x2), pbc(omka2), ALU.add, ["x1", "omka2"], ["x1"])
                    tt("gpsimd", sv(bkS, 64), gv(kT), gv(x2), ALU.mult, ["kT", "x1"], ["bkS"])
                    yield
                    stt("vector", sv(bkS, 0), gv(nkk), -1.0, gv(aT), ALU.mult, ALU.mult, ["nkk", "aT"], ["bkS"])
                    yield
                    tt("gpsimd", gv(x2), gv(rT), sv(bkS, 64), ALU.mult, ["rT", "bkS"], ["x1"])
                    tt("gpsimd", gv(x2), gv(x2), pbc(rk2), ALU.mult, ["x1", "rk2"], ["x1"])
                    yield
                    pb, pk = psum()
                    mm(pb[:, 0:4 * nt].rearrange("p (b t) -> p b t", t=nt), bones[:], x2[:, :, 0:nt], True, True, ["bones", "x1"], [pk])
                    pbv = pb[:, 0:256].rearrange("p (b s) -> p b s", s=64) if smp else pb[:, 0:512].rearrange("p (q s) -> p q s", s=64)
                    tt("vector", gv(bonT), pbv, sv(svS, 64), ALU.mult, [pk, "svS"], ["bonT"])
                    yield
                    act(zT[:, :, 0:nt], zT[:, :, 0:nt], AF.Silu, ["zT"], ["zT"])


                def gnorm(g):
                    smp = g == 16
                    nt = 64 if smp else 128
                    nch = 1 if smp else 2
                    m = 1 if smp else 0
                    Q = 4 if smp else 8
                    p8 = slice(8 * hp, 8 * hp + 8, 2) if smp else slice(8 * hp, 8 * hp + 8)

                    def gv(t, smp=smp):
                        return t[:, :, 0:64] if smp else t[:, :, :].rearrange("p b (c s) -> p (b c) s", s=64)

                    def sv(S, lo, smp=smp):
                        return S[:, 0:8:2, lo:lo + 64] if smp else S[:, :, lo:lo + 64]

                    def pbc(t, p8=p8, Q=Q):
                        return t[:, p8].unsqueeze(2).to_broadcast([128, Q, 64])

                    ntq = 4 * nt
                    r3 = lambda p_: p_[:, 0:ntq].rearrange("p (b t) -> p b t", t=nt)
                    pb, pk = psum()
                    mm(r3(pb), bones[:], ysT[:, :, 0:nt], True, True, ["bones", "ysTh0", "ysTh1"], [pk])
                    stt("vector", cen[:, :, 0:nt], r3(pb), -1.0 / 64, ysT[:, :, 0:nt], ALU.mult, ALU.add, [pk, "ysTh0", "ysTh1"], ["gcen"])
                    yield
                    tt("gpsimd", gx1[:, :, 0:nt], cen[:, :, 0:nt], cen[:, :, 0:nt], ALU.mult, ["gcen"], ["gx1"])
                    yield
                    pc, kc = psum()
                    for bi in range(4):
                        mm(pc[0:nt, 2 * bi:2 * bi + 2], gx1[:, bi, 0:nt], pm[:, 0:2], True, True, ["gx1", "pm"], [kc])
                    act(rnt[1][0:nt, :], pc[0:nt, 0:8], AF.Sqrt, [kc, "epsg"], ["rnt1"], bias=epsg[0:nt, 0:1], scale=1.0 / 64)
                    yield
                    op("vector", lambda e, nt=nt: e.reciprocal(out=rnt[1][0:nt, :], in_=rnt[1][0:nt, :]), ["rnt1"], ["rnt1"])
                    pt, kt = psum()
                    tr(pt[0:8, 0:nt], rnt[1][0:nt, :], ident[0:nt, 0:nt], ["rnt1", "ident"], [kt])
                    cp("scalar", rnT[1][:, 0:nt], pt[0:8, 0:nt], [kt], ["rnT1"])
                    yield
                    pb, pk = psum()
                    for bi in range(4):
                        mm(pb[:, bi * nt:(bi + 1) * nt], selT[:, bi, :], rnT[1][:, 0:nt], True, True, ["selT", "rnT1"], [pk])
                    pbn = pb[:, 0:256].rearrange("p (b s) -> p b s", s=64) if smp else pb[:, 0:512].rearrange("p (q s) -> p q s", s=64)
                    tt("vector", gv(cen), gv(cen), pbn, ALU.mult, ["gcen", pk], ["gcen"])
                    yield
                    tt("gpsimd", gv(cen), gv(cen), pbc(lnw2), ALU.mult, ["gcen", "lnw2"], ["gcen"])
                    yield
                    tt("gpsimd", gv(cen), gv(cen), pbc(lnb2), ALU.add, ["gcen", "lnb2"], ["gcen"])
                    yield
                    tt("gpsimd", gv(cen), gv(cen), gv(bonT), ALU.add, ["gcen", "bonT"], ["gcen"])
                    yield
                    oc = slice(SEQ, SEQ + 64) if smp else slice(g * 128, g * 128 + 128)
                    tt("vector", yrg[:, :, 0:nt], cen[:, :, 0:nt], zT[:, :, 0:nt], ALU.mult, ["gcen", "zT"], ["yrg"])
                    dma(yr_d[:, 4 * hp:4 * hp + 4, oc], yrg[:, :, 0:nt], r=["yrg"], dkey="yrg%d" % P.par)
                    yield

                def drain(gen, par):
                    old = P.par
                    P.par = par
                    for _ in gen:
                        pass
                    P.par = old

                st8 = {"gn": iter(()), "gnpar": 0}
                drain(front(0), 0)
                for g in range(17):
                    P.par = g % 2
                    nxt = front(g + 1) if g < 16 else iter(())

                    fcnt = [0]

                    def pump(k, pref="gn", g=g, nxt=nxt, fcnt=fcnt):
                        old = P.par
                        for _ in range(k):
                            order = ("front", "gn") if (pref == "front" and fcnt[0] < 13) else ("gn", "front")
                            for q in order:
                                if q == "gn":
                                    P.par = st8["gnpar"]
                                    try:
                                        next(st8["gn"])
                                        break
                                    except StopIteration:
                                        continue
                                else:
                                    P.par = (g + 1) % 2
                                    try:
                                        next(nxt)
                                        fcnt[0] += 1
                                        break
                                    except StopIteration:
                                        continue
                        P.par = old
                    lim = 99
                    smp = g == 16
                    nt = 64 if smp else 128
                    nch = 1 if smp else 2
                    m = 1 if smp else 0
                    Q = 4 if smp else 8
                    p8 = slice(8 * hp, 8 * hp + 8, 2) if smp else slice(8 * hp, 8 * hp + 8)

                    def gv(t, smp=smp):
                        return t[:, :, 0:64] if smp else t[:, :, :].rearrange("p b (c s) -> p (b c) s", s=64)

                    def sv(S, lo, smp=smp):
                        return S[:, 0:8:2, lo:lo + 64] if smp else S[:, :, lo:lo + 64]

                    def pbc(t, p8=p8, Q=Q):
                        return t[:, p8].unsqueeze(2).to_broadcast([128, Q, 64])

                    if smp:
                        P.barrier()
                        pb, pk = psum()
                        for bi in range(4):
                            tr(pb[0:64, bi * 128:(bi + 1) * 128], H[:, bi, :], ident[:], ["Hh0", "Hh1", "ident"], [pk])
                        cp("vector", Sout[0][:], pb[0:64, :].rearrange("p (h k) -> p h k", k=64), [pk], ["Sout0"])
                        dma(O["nrp"][8 * hp:8 * hp + 8, :, :].rearrange("h v k -> v h k"), Sout[0][:], r=["Sout0"], dkey="o_nrs0")
                        for b in range(16):
                            s = b % 2
                            dma(S0v[s][:], I["srwkv"][b, 8 * hp:8 * hp + 8, :, :].rearrange("h v k -> v h k"), w=["S0v%d" % s],
                                dkey="S0v%d" % s)
                            pb, pk = psum()
                            for bi in range(4):
                                tr(pb[:, bi * 64:(bi + 1) * 64], S0v[s][:, 2 * bi:2 * bi + 2, :].rearrange("p a k -> p (a k)"),
                                   ident[0:64, 0:64], ["S0v%d" % s, "ident"], [pk])
                            evac(H0T[:, b, :, :], pb[:, 0:256].rearrange("p (b v) -> p b v", v=64), [pk], ["H0T"])
                        cp("vector", arena_b[:, 12288:16384], arena[:, 0:4096], ["H0T"], ["H0Tb"])

                    def prepA(ch):
                        cs = slice(ch * 64, (ch + 1) * 64)
                        pX, kX = psum()
                        for bi in range(4):
                            tr(pX[:, bi * 128:(bi + 1) * 128], svS[:, 2 * bi + ch, :], ident[:], ["svS", "ident"], [kX])
                        cp("scalar", sig_tm[:, :], pX[0:64, :], [kX], ["sig_tm"])
                        cp("vector", UV[64:128, :, :], pX[64:128, :].rearrange("p (h v) -> p h v", v=64), [kX], ["UVv"])
                        cp("scalar", VP[64:128, :, :], pX[64:128, :].rearrange("p (h v) -> p h v", v=64), [kX], ["UVv"])
                        yield
                        pC, kC = psum()
                        for bi in range(4):
                            mm(pC[:, bi * 128:(bi + 1) * 128], sig_tm[:, bi * 128:(bi + 1) * 128], Tcat[m][:], True, True,
                               ["sig_tm", "Tc", "Tcs"], [kC])
                        pC3 = pC[:].rearrange("p (b x) -> p b x", x=128)
                        act(eI[:], pC3[:, :, 0:64], AF.Exp, [kC], ["eI"])
                        act(eNI[:], pC3[:, :, 0:64], AF.Exp, [kC], ["eNI"], scale=-1.0)
                        act(eE[:], pC3[:, :, 64:128], AF.Exp, [kC], ["eE"])
                        yield
                        pS, kS = psum()
                        mm(pS[:], Tsuf[m][:], sig_tm[:], True, True, ["sig_tm", "Tsf", "Tsfs"], [kS])
                        act(esuf[:], pS[:], AF.Exp, [kS], ["esuf"])
                        pY, kY = psum()
                        for bi in range(4):
                            tr(pY[:, bi * 128:(bi + 1) * 128], bkS[:, 2 * bi + ch, :], ident[:], ["bkS", "ident"], [kY])
                        tt("vector", BKh[:], pY[:], esuf[:], ALU.mult, [kY, "esuf"], ["BKh"])
                        yield
                        tt("gpsimd", AR[:, :, 0:64], nkk[:, :, cs], eE[:], ALU.mult, ["nkk", "eE"], ["ARa"])
                        tt("gpsimd", AR[:, :, 64:128], rT[:, :, cs], eI[:], ALU.mult, ["rT", "eI"], ["ARr"])
                        for par in range(2):
                            ps_ = slice(par * 64, par * 64 + 64)
                            tt("vector", BKtP[par][ps_, :, 0:64], bkS[ps_, ch:8:2, 0:64], eNI[ps_], ALU.mult, ["bkS", "eNI"], ["BKt"])
                            tt("vector", BKtP[par][ps_, :, 64:128], bkS[ps_, ch:8:2, 64:128], eNI[ps_], ALU.mult, ["bkS", "eNI"], ["BKt"])
                        yield
                        for q in range(2):
                            pA, kA = psum()
                            for j in range(4):
                                hl = 4 * q + j
                                bi, par = hl // 2, hl % 2
                                ps_ = slice(par * 64, par * 64 + 64)
                                mm(pA[:, j * 128:(j + 1) * 128], BKtP[par][:, bi, :], AR[:, bi, :], True, True, ["BKt", "ARa", "ARr"], [kA])
                            tt("vector", ATm[:, 4 * q:4 * q + 4, :], pA[:].rearrange("p (h x) -> p h x", x=128),
                               maskT[m][:].unsqueeze(1).to_broadcast([128, 4, 128]), ALU.mult, [kA, "mT", "mTs"], ["ATm"])
                            yield
                        pN, kN = psum()
                        for hl in range(8):
                            bi, par = hl // 2, hl % 2
                            ps_ = slice(par * 64, par * 64 + 64)
                            mm(pN[0:64, hl * 64:(hl + 1) * 64], AR[:, bi, 0:64], BKtP[par][:, bi, 0:64], True, True, ["ARa", "BKt"], [kN])
                        tt("vector", Xb[0][:, :, ch, :], pN[0:64, :].rearrange("p (h x) -> p h x", x=64),
                           maskN[m][:].unsqueeze(1).to_broadcast([64, 8, 64]), ALU.mult, [kN, "mN", "mNs"], ["X0h0", "X0h1"])
                        tt("vector", TTb[0][:, :, ch, :], ATm[0:64, :, 0:64], ident[0:64, 0:64].unsqueeze(1).to_broadcast([64, 8, 64]), ALU.add,
                           ["ATm", "ident"], ["TT0h0", "TT0h1"])
                        cp("scalar", XTb[1][:, :, ch, :], ATm[0:64, :, 0:64], ["ATm"], ["XT1h0", "XT1h1"])

                    def powers():
                        Xc, XTc, TTc = Xb[0], XTb[1], TTb[0]
                        kXc, kXTc, kTTc = "X0", "XT1", "TT0"
                        ck = lambda base: [base + "#%d" % c_ for c_ in range(nch)]
                        for lvl in range(5):
                            Xn = Xb[(lvl + 1) % 2]; kXn = "X%d" % ((lvl + 1) % 2)
                            TTn = TTb[(lvl + 1) % 2]; kTTn = "TT%d" % ((lvl + 1) % 2)
                            if lvl < 4:
                                XTn = XTb[lvl % 2]; kXTn = "XT%d" % (lvl % 2)
                            for hh in range(2):
                                hs = slice(4 * hh, 4 * hh + 4)
                                sfx = "h%d" % hh
                                pv = lambda p_: p_[0:64, 0:256 * nch].rearrange("p (h c x) -> p h c x", c=nch, x=64)
                                col = lambda hl, c_: slice(((hl % 4) * nch + c_) * 64, ((hl % 4) * nch + c_ + 1) * 64)
                                p1, k1_ = psum()
                                for c_ in range(nch):
                                    for hl in range(4 * hh, 4 * hh + 4):
                                        mm(p1[0:64, col(hl, c_)], XTc[:, hl, c_, :], Xc[:, hl, c_, :], True, True,
                                           [kXTc + sfx + "#%d" % c_, kXc + sfx + "#%d" % c_], [k1_])
                                cp("scalar", Xn[:, hs, 0:nch, :], pv(p1), [k1_], ck(kXn + sfx))
                                if lvl < 4:
                                    p2, k2_ = psum()
                                    for c_ in range(nch):
                                        for hl in range(4 * hh, 4 * hh + 4):
                                            mm(p2[0:64, col(hl, c_)], Xc[:, hl, c_, :], XTc[:, hl, c_, :], True, True,
                                               [kXTc + sfx + "#%d" % c_, kXc + sfx + "#%d" % c_], [k2_])
                                    cp("vector", XTn[:, hs, 0:nch, :], pv(p2), [k2_], ck(kXTn + sfx))
                            yield
                            for hh in range(2):
                                hs = slice(4 * hh, 4 * hh + 4)
                                sfx = "h%d" % hh
                                pv = lambda p_: p_[0:64, 0:256 * nch].rearrange("p (h c x) -> p h c x", c=nch, x=64)
                                col = lambda hl, c_: slice(((hl % 4) * nch + c_) * 64, ((hl % 4) * nch + c_ + 1) * 64)
                                p3, k3_ = psum()
                                for c_ in range(nch):
                                    for hl in range(4 * hh, 4 * hh + 4):
                                        mm(p3[0:64, col(hl, c_)], Xn[:, hl, c_, :], TTc[:, hl, c_, :], True, True,
                                           [kXn + sfx + "#%d" % c_, kTTc + sfx + "#%d" % c_], [k3_])
                                tt("vector", TTn[:, hs, 0:nch, :], pv(p3), TTc[:, hs, 0:nch, :], ALU.add,
                                   [k3_] + ck(kTTc + sfx), ck(kTTn + sfx))
                            yield
                            Xc, kXc = Xn, kXn
                            if lvl < 4:
                                XTc, kXTc = XTn, kXTn
                            TTc, kTTc = TTn, kTTn

                    def stateB(ch):
                        cs = slice(ch * 64, (ch + 1) * 64)
                        if not smp:
                            for hh in range(2):
                                bs = slice(2 * hh, 2 * hh + 2)
                                tt("gpsimd", HP[:, bs], H[:, bs], eI[:, bs, 63:64].to_broadcast([128, 2, 64]), ALU.mult,
                                   ["Hh%d" % hh, "eI"], ["HPh%d" % hh])
                        TTc, kTTc = TTb[1][:, :, ch, :], "TT1"
                        def st_W(hh):
                            sfx = "h%d" % hh
                            pW, kW = psum()
                            for hl in range(4 * hh, 4 * hh + 4):
                                bi, par = hl // 2, hl % 2
                                o_ = pW[0:64, (hl % 4) * 64:(hl % 4 + 1) * 64]
                                if smp:
                                    ma, kma = ARm[hl % 2], "ARm%d" % (hl % 2)
                                    stt("vector", ma[:], AR[:, bi, 0:64].unsqueeze(1).to_broadcast([128, 16, 64]), pm[:, par:par + 1], cmask[:],
                                        ALU.mult, ALU.mult, ["ARa", "cmask", "pm"], [kma])
                                    for b in range(16):
                                        mm(o_, ma[:, b, :], H0Tb[:, b, bi, :], b == 0, False, [kma, "H0Tb"], [kW])
                                else:
                                    mm(o_, AR[:, bi, 0:64], HPd[par][:, bi, :], True, False, ["ARa", "HPd" + sfx], [kW])
                                mm(o_, ATm[:, hl, 0:64], VP[:, hl, :], False, True, ["ATm", "UVv"], [kW])
                            cp("scalar", Wsb[:, 4 * hh:4 * hh + 4, :], pW[0:64, 0:256].rearrange("p (h x) -> p h x", x=64), [kW], ["Wsb" + sfx])

                        def st_U(hh):
                            sfx = "h%d" % hh
                            pU, kU = psum()
                            for hl in range(4 * hh, 4 * hh + 4):
                                mm(pU[0:64, (hl % 4) * 64:(hl % 4 + 1) * 64], TTc[:, hl, :], Wsb[:, hl, :], True, True, [kTTc + sfx, "Wsb" + sfx], [kU])
                            cp("vector", UV[0:64, 4 * hh:4 * hh + 4, :], pU[0:64, 0:256].rearrange("p (h x) -> p h x", x=64), [kU], ["UVu" + sfx])

                        def st_Y(hh):
                            sfx = "h%d" % hh
                            pYs, kYs = psum()
                            for hl in range(4 * hh, 4 * hh + 4):
                                bi, par = hl // 2, hl % 2
                                ps_ = slice(par * 64, par * 64 + 64)
                                o_ = pYs[ps_, (bi % 2) * 64:(bi % 2 + 1) * 64]
                                if smp:
                                    mr, kmr = ARm[2 + hl % 2], "ARm%d" % (2 + hl % 2)
                                    stt("vector", mr[:], AR[:, bi, 64:128].unsqueeze(1).to_broadcast([128, 16, 64]), pm[:, par:par + 1], cmask[:],
                                        ALU.mult, ALU.mult, ["ARr", "cmask", "pm"], [kmr])
                                    for b in range(16):
                                        mm(o_, H0Tb[:, b, bi, :], mr[:, b, :], b == 0, False, [kmr, "H0Tb"], [kYs])
                                else:
                                    mm(o_, HPd[par][:, bi, :], AR[:, bi, 64:128], True, False, ["ARr", "HPd" + sfx], [kYs])
                                mm(o_, UV[:, hl, :], ATm[:, hl, 64:128], False, True, ["UVu" + sfx, "UVv", "ATm"], [kYs])
                            cp("scalar", ysT[:, 2 * hh:2 * hh + 2, cs], pYs[:, 0:128].rearrange("p (b s) -> p b s", s=64), [kYs], ["ysT" + sfx])

                        def st_H(hh):
                            sfx = "h%d" % hh
                            bs = slice(2 * hh, 2 * hh + 2)
                            pH, kH = psum()
                            for hl in range(4 * hh, 4 * hh + 4):
                                bi, par = hl // 2, hl % 2
                                ps_ = slice(par * 64, par * 64 + 64)
                                mm(pH[ps_, (bi % 2) * 64:(bi % 2 + 1) * 64], BKh[:, hl * 64:(hl + 1) * 64], UV[:, hl, :], True, True,
                                   ["BKh", "UVu" + sfx, "UVv"], [kH])
                            pH3 = pH[:, 0:128].rearrange("p (b v) -> p b v", v=64)
                            tt("vector", HPd[0][0:64, bs], pH3[0:64], HP[0:64, bs], ALU.add, [kH, "HP" + sfx], ["HPd" + sfx])
                            tt("vector", HPd[1][64:128, bs], pH3[64:128], HP[64:128, bs], ALU.add, [kH, "HP" + sfx], ["HPd" + sfx])
                            tt("vector", H[:, bs], pH3, HP[:, bs], ALU.add, [kH, "HP" + sfx], ["H" + sfx])

                        st_W(0); pump(1); st_W(1); pump(1); st_U(0); pump(1); st_U(1); pump(1)
                        if not smp:
                            st_Y(0); st_H(0); pump(1); st_Y(1); st_H(1); pump(1)
                        else:
                            st_Y(0); st_Y(1)
                        if smp:
                            for bi in range(4):
                                tt("gpsimd", H0T[:, :, bi, :], H0T[:, :, bi, :], eI[:, bi, 3:64:4].unsqueeze(2).to_broadcast([128, 16, 64]),
                                   ALU.mult, ["H0T", "eI"], ["H0T"])
                            for b in range(16):
                                s = b % 2
                                op("scalar", lambda e, s=s, b=b: e.mul(out=UVm[s][:], in_=UV[:], mul=bd4[:, 4 * b:4 * b + 1]),
                                   ["UVuh0", "UVuh1", "UVv", "bd4"], ["UVm%d" % s])
                                pF, kF = psum()
                                pG, kG = psum()
                                for hl in range(8):
                                    bi, par = hl // 2, hl % 2
                                    mm(pF[0:64, hl * 64:(hl + 1) * 64], H0T[:, b, bi, :], idP[par][:], True, True, ["H0T", "idP"], [kF])
                                for hl in range(8):
                                    mm(pG[0:64, hl * 64:(hl + 1) * 64], UVm[s][:, hl, :], BKh[:, hl * 64:(hl + 1) * 64], True, True,
                                       ["UVm%d" % s, "BKh"], [kG])
                                cp("scalar", Sout[s][:], pF[0:64, :].rearrange("p (h k) -> p h k", k=64), [kF], ["Sout%d" % s])
                                tt("vector", Sout[s][:], pG[0:64, :].rearrange("p (h k) -> p h k", k=64), Sout[s][:], ALU.add,
                                   [kG, "Sout%d" % s], ["Sout%d" % s])
                                dma(O["nrs"][b, 8 * hp:8 * hp + 8, :, :].rearrange("h v k -> v h k"), Sout[s][:], r=["Sout%d" % s],
                                    dkey="o_nrs%d" % s)

                    gens = [prepA(c_) for c_ in range(nch)]
                    alive = [True] * nch
                    while any(alive):
                        for c_ in range(nch):
                            if alive[c_]:
                                P.cpar = c_
                                try:
                                    next(gens[c_])
                                except StopIteration:
                                    alive[c_] = False
                        pump(1, "front")
                    for _ in powers():
                        pump(1, "gn")
                    for c_ in range(nch):
                        P.cpar = c_
                        stateB(c_)
                    pump(1000)
                    st8["gn"], st8["gnpar"] = gnorm(g), g % 2
                drain(st8["gn"], st8["gnpar"])
                P.barrier()
        sAr.close()
        if stage >= 3:
          with ExitStack() as sA:
            sbA = lambda name, shape, dt=F32: sb(name, shape, dt, st=sA)
            mT = sbA("mT", [128, 8, NTOK], BF16)
            TCH = [(1 + 512 * q, 512 * q, 512) for q in range(4)] + [(SC0, SEQ, 64)]
            with ExitStack() as sB:
                sbB = lambda name, shape, dt=F32: sb(name, shape, dt, st=sB)
                ycT = sbB("ycT", [128, 8, NTOK], BF16)
                yrT = sbB("yrT", [128, 8, NTOK], BF16)
                wcs = [sbB("wcs%d" % i, [128, 8, 256]) for i in range(2)]
                wcb = [sbB("wcb%d" % i, [128, 8, 512], BF16) for i in range(2)]
                wcount = {"n": 0}

                def load4(srcs):
                    i = wcount["n"] % 2
                    wcount["n"] += 1
                    for h in range(2):
                        sl = (2 * wcount["n"] + h) % 2
                        kw = "wcs%d" % sl
                        for j in range(2):
                            dma(wcs[sl][:, :, j * 128:(j + 1) * 128], srcs[2 * h + j].rearrange("(d p) c -> p d c", p=128), w=[kw], dkey=kw,
                                eng="sync" if j == 0 else "gpsimd")
                        cp("scalar" if h == 0 else "vector", wcb[i][:, :, h * 256:(h + 1) * 256], wcs[sl][:], [kw], ["wcb%d" % i])
                    return wcb[i], "wcb%d" % i

                with ExitStack() as sC:
                    sbC = lambda name, shape, dt=F32: sb(name, shape, dt, st=sC)
                    ub = sbC("ub", [128, 2 + SEQ])
                    us = sbC("us", [128, 16, 6])
                    tmpx = sbC("tmpx", [128, 512]); acc = sbC("acc", [128, 512]); sz = sbC("sz", [128, 512]); t2 = sbC("t2", [128, 512])
                    ncst = sbC("ncst", [128, 8, 34]); ncrow = sbC("ncrow", [34, D])
                    conv_srcs = lambda cb: [I["w_in"][:, base + cb * 128:base + (cb + 1) * 128] for base in (CB_XIN, CB_B, CB_C, CB_ZC)]
                    tail_srcs = lambda db: [I["w_out_c"][:, db * 128:(db + 1) * 128], I["w_out_r"][:, db * 128:(db + 1) * 128],
                                            I["w_in"][:, CB_GC + db * 128:CB_GC + (db + 1) * 128],
                                            I["w_in"][:, CB_GR + db * 128:CB_GR + (db + 1) * 128]]
                    wnext = load4(conv_srcs(0))
                    dma(yrT[:], yr_d, w=["yrT"], dkey="yrT")
                    for cb in range(8):
                        wt, kwt = wnext
                        mset("gpsimd", ub[:, 0:2], 0.0, ["ub"])
                        cp("gpsimd", us[:, :, 0:2], ucb[:, cb, :].rearrange("p (b j) -> p b j", j=2), ["us"], ["us"])
                        cw = lambda j: prm[:, PR_CW + 8 * j + cb:PR_CW + 8 * j + cb + 1]
                        for ci_, (hc, oc, n) in enumerate(TCH):
                            if ci_ == 1:
                                wnext = load4(conv_srcs(cb + 1)) if cb < 7 else (load4(tail_srcs(0)) if stage >= 4 else None)
                            smp = n == 64
                            pp_ = [psum() for _ in range(4)]
                            for gi in range(4):
                                for db in range(8):
                                    mm(pp_[gi][0][:, 0:n], wt[:, db, gi * 128:(gi + 1) * 128], hT[:, db, hc:hc + n], db == 0, db == 7, [kwt], [pp_[gi][1]])
                            (pxin, kxin), (pbg, kbg), (pcg, kcg), (pzc, kzc) = pp_
                            cp("scalar", tmpx[:, 0:n], pxin[:, 0:n], [kxin], ["tmpx"])
                            if not smp:
                                tt("vector", ub[:, 2 + oc:2 + oc + n], pcg[:, 0:n], tmpx[:, 0:n], ALU.mult, [kcg, "tmpx", "ub"], ["ub"])
                                w_ = lambda j: ub[:, oc + j:oc + j + n]
                                a_ = acc[:, 0:n]
                            else:
                                tt("vector", us[:, :, 2:6], pcg[:, 0:64].rearrange("p (b t) -> p b t", t=4),
                                   tmpx[:, 0:64].rearrange("p (b t) -> p b t", t=4), ALU.mult, [kcg, "tmpx", "us"], ["us"])
                                w_ = lambda j: us[:, :, j:j + 4]
                                a_ = acc[:, 0:64].rearrange("p (b t) -> p b t", t=4)
                            ku = "us" if smp else "ub"
                            op("scalar", lambda e, a_=a_, w2=w_(2), c2=cw(2): e.mul(out=a_, in_=w2, mul=c2), [ku, "prm"], ["acc"])
                            stt("vector", a_, w_(1), cw(1), a_, ALU.mult, ALU.add, [ku, "acc", "prm"], ["acc"])
                            stt("vector", a_, w_(0), cw(0), a_, ALU.mult, ALU.add, [ku, "acc", "prm"], ["acc"])
                            act(sz[:, 0:n], pzc[:, 0:n], AF.Silu, [kzc], ["sz"])
                            tt("vector", t2[:, 0:n], pbg[:, 0:n], acc[:, 0:n], ALU.mult, [kbg, "acc"], ["t2"])
                            tt("gpsimd", ycT[:, cb, oc:oc + n], t2[:, 0:n], sz[:, 0:n], ALU.mult, ["t2", "sz"], ["ycT"])
                        cp("gpsimd", ncst[:, cb, 0:32].rearrange("p (b j) -> p b j", j=2), us[:, :, 4:6], ["us", "ncst"], ["ncst"])
                        cp("gpsimd", ncst[:, cb, 32:34], ub[:, SEQ:SEQ + 2], ["ub", "ncst"], ["ncst"])
                    pa, ka = psum()
                    pb, kb = psum()
                    for cb in range(8):
                        pp, kk_ = (pa, ka) if cb < 4 else (pb, kb)
                        tr(pp[0:34, (cb % 4) * 128:(cb % 4 + 1) * 128], ncst[:, cb, :], ident[:], ["ncst", "ident"], [kk_])
                    cp("scalar", ncrow[:, 0:512], pa[0:34, :], [ka], ["ncrow"])
                    cp("vector", ncrow[:, 512:1024], pb[0:34, :], [kb], ["ncrow"])
                    dma(O["ncs"], ncrow[0:32, :], r=["ncrow"], dkey="o_ncs")
                    dma(O["ncp"], ncrow[32:34, :], r=["ncrow"], dkey="o_ncp")
                    P.barrier()
                if stage >= 4:
                  with ExitStack() as sD:
                    sbD = lambda name, shape, dt=F32: sb(name, shape, dt, st=sD)
                    sgc = sbD("sgc", [128, 512]); sgr = sbD("sgr", [128, 512]); t1_ = sbD("t1_", [128, 512]); t2_ = sbD("t2_", [128, 512])
                    for db in range(8):
                        wt, kwt = wnext
                        for ci_, (hc, oc, n) in enumerate(TCH):
                            if ci_ == 1 and db < 7:
                                wnext = load4(tail_srcs(db + 1))
                            pp_ = [psum() for _ in range(4)]
                            srcs = (lambda c_: ycT[:, c_, oc:oc + n], lambda c_: yrT[:, c_, oc:oc + n],
                                    lambda c_: hT[:, c_, hc:hc + n], lambda c_: hT[:, c_, hc:hc + n])
                            for gi in range(4):
                                for c_ in range(8):
                                    mm(pp_[gi][0][:, 0:n], wt[:, c_, gi * 128:(gi + 1) * 128], srcs[gi](c_), c_ == 0, c_ == 7,
                                       [kwt, "ycT", "yrT"], [pp_[gi][1]])
                            (ppc, kpc), (ppr, kpr), (pgc, kgc), (pgr, kgr) = pp_
                            act(sgc[:, 0:n], pgc[:, 0:n], AF.Sigmoid, [kgc], ["sgc"])
                            act(sgr[:, 0:n], pgr[:, 0:n], AF.Sigmoid, [kgr], ["sgr"])
                            tt("vector", t1_[:, 0:n], ppc[:, 0:n], sgc[:, 0:n], ALU.mult, [kpc, "sgc"], ["t1_"])
                            tt("vector", t2_[:, 0:n], ppr[:, 0:n], sgr[:, 0:n], ALU.mult, [kpr, "sgr"], ["t2_"])
                            tt("gpsimd", mT[:, db, oc:oc + n], t1_[:, 0:n], t2_[:, 0:n], ALU.add, ["t1_", "t2_"], ["mT"])
                    P.barrier()
            if stage >= 4:
              with ExitStack() as sE:
                sbE = lambda name, shape, dt=F32: sb(name, shape, dt, st=sE)
                wo = sbE("wo", [128, 8, D], BF16)
                wos = [sbE("wos%d" % i, [128, 8, 256]) for i in range(2)]
                fgc = sbE("fgc", [128, D])
                xt2 = [sbE("xt2_%d" % i, [128, D]) for i in range(4)]
                res = [sbE("res%d" % i, [128, D]) for i in range(2)]
                yo = [sbE("yo%d" % i, [128, D]) for i in range(2)]
                junk2 = sbE("junk2", [128, D])
                ssum2 = [sbE("ssum2_%d" % i, [128, 1]) for i in range(2)]
                rstd2 = [sbE("rstd2_%d" % i, [128, 1]) for i in range(2)]
                dma(fgc[:], I["final_g"].partition_broadcast(128), w=["fgc"], dkey="fgc")
                for q in range(4):
                    sl = q % 2
                    kw = "wos%d" % sl
                    dma(wos[sl][:], I["w_o"][:, q * 256:(q + 1) * 256].rearrange("(d p) c -> p d c", p=128), w=[kw], dkey=kw)
                    cp("scalar" if q % 2 == 0 else "vector", wo[:, :, q * 256:(q + 1) * 256], wos[sl][:], [kw], ["wo"])
                def load_x(j):
                    if j < 17:
                        src = I["xs"] if j == 16 else I["xp"][j * 128:(j + 1) * 128, :]
                        dma(xt2[j % 4][0:(64 if j == 16 else 128), :], src, w=["xt2_%d" % (j % 4)], dkey="xt2_%d" % (j % 4))

                for j in range(3):
                    load_x(j)
                for i in range(17):
                    s = i % 2
                    nr = 64 if i == 16 else 128
                    oc = SEQ if i == 16 else i * 128
                    sx = i % 4
                    kx, kr_, ky, ks, krs = "xt2_%d" % sx, "res%d" % s, "yo%d" % s, "ss2_%d" % s, "rs2_%d" % s
                    load_x(i + 3)
                    pa, ka = psum()
                    pb, kb = psum()
                    for hf, (pp, kk_) in enumerate(((pa, ka), (pb, kb))):
                        for db in range(8):
                            mm(pp[0:nr, :], mT[:, db, oc:oc + nr], wo[:, db, hf * 512:(hf + 1) * 512], db == 0, db == 7, ["wo", "mT"], [kk_])
                    tt("vector", res[s][0:nr, 0:512], pa[0:nr, :], xt2[sx][0:nr, 0:512], ALU.add, [ka, kx], [kr_])
                    tt("vector", res[s][0:nr, 512:1024], pb[0:nr, :], xt2[sx][0:nr, 512:1024], ALU.add, [kb, kx], [kr_])
                    mset("gpsimd", ssum2[s][:], 0.0, [ks])
                    act(junk2[0:nr, :], res[s][0:nr, :], AF.Square, [kr_, ks], [ks], accum=ssum2[s][0:nr, :])
                    ts("vector", rstd2[s][0:nr, :], ssum2[s][0:nr, :], 1.0 / D, RMS_EPS, ALU.mult, ALU.add, [ks], [krs])
                    act(rstd2[s][0:nr, :], rstd2[s][0:nr, :], AF.Sqrt, [krs], [krs])
                    op("vector", lambda e, s=s, nr=nr: e.reciprocal(out=rstd2[s][0:nr, :], in_=rstd2[s][0:nr, :]), [krs], [krs])
                    stt("vector", yo[s][0:nr, :], res[s][0:nr, :], rstd2[s][0:nr, 0:1], fgc[0:nr, :], ALU.mult, ALU.mult, [kr_, krs, "fgc"], [ky])
                    dst = O["ys"] if i == 16 else O["yp"][i * 128:(i + 1) * 128, :]
                    dma(dst, yo[s][0:nr, :], r=[ky], dkey=ky)

        P.finish()
        nsem = P.emit()
    return nc, nsem


def shard_inputs(inp, core):
    f = lambda a: np.ascontiguousarray(np.asarray(a, dtype=np.float32))
    b0 = core * NSB
    m = {
        "xp": f(inp["x_prompt"][core]),
        "xs": f(inp["x_sample"][b0:b0 + NSB]).reshape(64, D),
        "sconv": f(inp["state_conv"][0, b0:b0 + NSB]).reshape(32, D),
        "sshift": f(inp["state_shift"][0, b0:b0 + NSB]),
        "srwkv": f(inp["state_rwkv"][0, b0:b0 + NSB]),
        "norm_g": f(inp["norm_g"][0]), "w_in": f(inp["w_in"][0]), "conv_w": f(inp["conv_w"][0]),
        "mu": f(inp["mu_shift"][0]), "w0": f(inp["w0"][0]), "w_up": f(inp["w_up"][0]), "a0": f(inp["a0"][0]),
        "a_up": f(inp["a_up"][0]), "k_k": f(inp["k_k"][0]), "k_a": f(inp["k_a"][0]),
        "r_k": f(inp["r_k"][0]).reshape(D), "ln_w": f(inp["ln_w"][0]), "ln_b": f(inp["ln_b"][0]),
        "w_out_c": f(inp["w_out_c"][0]), "w_out_r": f(inp["w_out_r"][0]), "w_o": f(inp["w_o"][0]),
        "final_g": f(inp["final_g"]),
    }
    return m


_CACHE = {}


def kernel(**inputs):
    if "nc" not in _CACHE:
        _CACHE["nc"] = build_program()[0]
    nc = _CACHE["nc"]
    in_maps = [shard_inputs(inputs, c) for c in range(8)]
    res = run_bass_kernel_spmd(nc, in_maps, core_ids=list(range(8)))
    R = res.results
    yp = np.stack([R[c]["yp"] for c in range(8)])
    ys = np.concatenate([R[c]["ys"].reshape(NSB, 4, D) for c in range(8)])
    ncp = np.stack([R[c]["ncp"] for c in range(8)])[None]
    nsp = np.stack([R[c]["nsp"].reshape(D) for c in range(8)])[None]
    nrp = np.stack([R[c]["nrp"] for c in range(8)])[None]
    ncs = np.concatenate([R[c]["ncs"].reshape(NSB, 2, D) for c in range(8)])[None]
    nss = np.concatenate([R[c]["nss"] for c in range(8)])[None]
    nrs = np.concatenate([R[c]["nrs"] for c in range(8)])[None]
    return tuple(np.ascontiguousarray(a, dtype=np.float32) for a in (yp, ys, ncp, nsp, nrp, ncs, nss, nrs))
```
